# Optimizing a Trainium2 kernel written in Bass

```python
import math
import jax, jax.numpy as jnp
from jax import lax
import numpy as np


D_MODEL = 2048
BATCH = 16
SEQ = 2048
DEPTH = 1
DEC_BATCH = 4
DEC_SEQ = 8192
PAST_LEN = 128

N_MEM = 256
A_WIDTH = D_MODEL // 2
A_HALF_DIM = 64
A_HEADS = A_WIDTH // (2 * A_HALF_DIM)
A_VDIM = 2 * A_HALF_DIM
B_WIDTH = D_MODEL // 2
B_DIM = 128
B_HEADS = B_WIDTH // B_DIM
CHUNK = 64
REL_BUCKETS = 32
REL_MAX_DIST = 128
Q_BLOCK = 128
X_HEADS = 4
X_DIM = 128
X_WIDTH = X_HEADS * X_DIM
D_FF = -(-8 * D_MODEL // (3 * 256)) * 256
RMS_EPS = 1e-6
IN_SIZES = (A_WIDTH, A_WIDTH, A_WIDTH, B_WIDTH, B_WIDTH, B_WIDTH, B_WIDTH, B_WIDTH, 2 * D_MODEL)
IN_SPLITS = tuple(int(c) for c in np.cumsum(IN_SIZES)[:-1])
N_IN = int(sum(IN_SIZES))

kernel_name = 'hybrid_diffattn_hgrn2_encoder'


def rmsnorm(x, g, eps=RMS_EPS):
    xf = x.astype(jnp.float32)
    y = xf * lax.rsqrt(jnp.mean(xf * xf, axis=-1, keepdims=True) + eps)
    return (y * g.astype(jnp.float32)).astype(x.dtype)


def rel_bucket(rel):
    nb = REL_BUCKETS // 2
    max_exact = nb // 2
    n = jnp.abs(rel)
    nf = jnp.maximum(n, 1).astype(jnp.float32)
    large = max_exact + (jnp.log(nf / max_exact) / math.log(REL_MAX_DIST / max_exact) * (nb - max_exact)).astype(jnp.int32)
    large = jnp.minimum(large, nb - 1)
    return jnp.where(rel > 0, nb, 0) + jnp.where(n < max_exact, n, large)


def diff_attention(q, k, v, rel_bias, lam, g_subln, lam_init):
    bsz, s_len = q.shape[0], q.shape[1]
    nblk = s_len // Q_BLOCK
    q_blocks = jnp.swapaxes(q.reshape(bsz, nblk, Q_BLOCK, A_HEADS, 2, A_HALF_DIM), 0, 1)
    k_pos = jnp.arange(s_len, dtype=jnp.int32)
    scale = A_HALF_DIM ** -0.5

    def block(args):
        q_blk, start = args
        q_pos = start + jnp.arange(Q_BLOCK, dtype=jnp.int32)
        bias = jnp.transpose(rel_bias[rel_bucket(k_pos[None, :] - q_pos[:, None])], (2, 0, 1)).astype(jnp.float32)
        logits = jnp.einsum('bqhcd,bkhcd->bhcqk', q_blk, k).astype(jnp.float32) * scale + bias[None, :, None]
        p = jax.nn.softmax(logits, axis=-1)
        w = p[:, :, 0] - lam * p[:, :, 1]
        return jnp.einsum('bhqk,bkhe->bqhe', w.astype(v.dtype), v)

    starts = jnp.arange(nblk, dtype=jnp.int32) * Q_BLOCK
    o = lax.map(block, (q_blocks, starts))
    o = jnp.swapaxes(o, 0, 1).reshape(bsz, s_len, A_HEADS, A_VDIM)
    o = rmsnorm(o, g_subln) * (1.0 - lam_init)
    return o.reshape(bsz, s_len, A_WIDTH)


def hgrn_chunk_scan(q, k, v, logf):
    bsz, s_len = q.shape[0], q.shape[1]
    n = s_len // CHUNK
    rs = lambda t: t.reshape(bsz, n, CHUNK, B_HEADS, t.shape[-1])
    q, k, v, logf = rs(q), rs(k), rs(v), rs(logf)
    b = jnp.cumsum(logf, axis=2)
    b_ref = b[:, :, CHUNK // 2 - 1:CHUNK // 2]
    b_last = b[:, :, -1:]
    a = jnp.einsum('bnthd,bnshd->bnhts', q * jnp.exp(b - b_ref), k * jnp.exp(b_ref - b))
    tri = jnp.tril(jnp.ones((CHUNK, CHUNK), dtype=bool))
    a = jnp.where(tri, a, 0.0)
    intra = jnp.einsum('bnhts,bnshv->bnthv', a, v)
    kv = jnp.einsum('bnshd,bnshv->bnhdv', k * jnp.exp(b_last - b), v)
    decay = jnp.exp(b_last[:, :, 0])

    def step(state, xs):
        kv_n, dec_n = xs
        return state * dec_n[..., None] + kv_n, state

    s0 = jnp.zeros((bsz, B_HEADS, B_DIM, v.shape[-1]), jnp.float32)
    _, s_prev = lax.scan(step, s0, (jnp.moveaxis(kv, 1, 0), jnp.moveaxis(decay, 1, 0)))
    s_prev = jnp.moveaxis(s_prev, 0, 1)
    inter = jnp.einsum('bnthd,bnhdv->bnthv', q * jnp.exp(b), s_prev)
    return (intra + inter).reshape(bsz, s_len, B_HEADS, v.shape[-1])


def hgrn2_branch(q_in, z_fwd, z_bwd, v_in, g_in, lb_fwd, lb_bwd, g_norm):
    bsz, s_len = q_in.shape[0], q_in.shape[1]
    heads = lambda t: t.reshape(bsz, s_len, B_HEADS, B_DIM)
    q = heads(q_in.astype(jnp.float32)) * B_DIM ** -0.5
    v = heads(v_in.astype(jnp.float32))

    def gates(z, lb):
        z = z.astype(jnp.float32)
        logf = jnp.log(lb + (1.0 - lb) * jax.nn.sigmoid(z))
        k = (1.0 - lb) * jax.nn.sigmoid(-z)
        return heads(k), heads(logf)

    k_f, logf_f = gates(z_fwd, lb_fwd)
    k_b, logf_b = gates(z_bwd, lb_bwd)
    flip = lambda t: jnp.flip(t, axis=1)
    o = hgrn_chunk_scan(q, k_f, v, logf_f) + flip(hgrn_chunk_scan(flip(q), flip(k_b), flip(v), flip(logf_b)))
    o = rmsnorm(o, g_norm) * jax.nn.silu(heads(g_in.astype(jnp.float32)))
    return o.reshape(bsz, s_len, B_WIDTH).astype(q_in.dtype)


def encoder_layer(x, mem, l, p, lb_all):
    dt = x.dtype
    bsz, s_len = x.shape[0], x.shape[1]
    h = rmsnorm(x, p['g_pre_mix'][l])
    proj = h @ p['w_in'][l]
    qa, ka, va, qb, zf, zb, vb, gb, gate_logits = jnp.split(proj, IN_SPLITS, axis=-1)
    lam_init = 0.8 - 0.6 * math.exp(-0.3 * l)
    lam = (jnp.exp(jnp.sum(p['lam_q1'][l].astype(jnp.float32) * p['lam_k1'][l].astype(jnp.float32)))
           - jnp.exp(jnp.sum(p['lam_q2'][l].astype(jnp.float32) * p['lam_k2'][l].astype(jnp.float32))) + lam_init)
    shp = (bsz, s_len, A_HEADS, 2, A_HALF_DIM)
    ya = diff_attention(qa.reshape(shp), ka.reshape(shp), va.reshape(bsz, s_len, A_HEADS, A_VDIM),
                        p['rel_bias'], lam, p['g_subln'][l], lam_init)
    yb = hgrn2_branch(qb, zf, zb, vb, gb, lb_all[0, l], lb_all[1, l], p['g_hgrn_norm'][l])
    g_a, g_b = jnp.split(jax.nn.sigmoid(gate_logits + p['b_merge'][l]), 2, axis=-1)
    mix = g_a * (ya @ p['w_branch_a'][l]) + g_b * (yb @ p['w_branch_b'][l])
    x = x + rmsnorm(mix @ p['w_out'][l], p['g_post_mix'][l])
    h = rmsnorm(x, p['g_pre_x'][l])
    m = rmsnorm(mem, p['g_mem'][l])
    qx = (h @ p['w_q_x'][l]).reshape(bsz, s_len, X_HEADS, X_DIM)
    kv = (m @ p['w_kv_x'][l]).reshape(bsz, mem.shape[1], 2, X_HEADS, X_DIM)
    logits = jnp.einsum('bqhd,bkhd->bhqk', qx, kv[:, :, 0]).astype(jnp.float32) * X_DIM ** -0.5
    att = jax.nn.softmax(logits, axis=-1).astype(dt)
    ox = jnp.einsum('bhqk,bkhd->bqhd', att, kv[:, :, 1]).reshape(bsz, s_len, X_WIDTH)
    x = x + rmsnorm(ox @ p['w_o_x'][l], p['g_post_x'][l])
    h = rmsnorm(x, p['g_pre_ffn'][l])
    gt, up = jnp.split(h @ p['w_gate_up'][l], 2, axis=-1)
    x = x + rmsnorm((jax.nn.silu(gt) * up) @ p['w_down'][l], p['g_post_ffn'][l])
    return x


def trunk(x, mem, p):
    lb_all = jnp.cumsum(jax.nn.softmax(p['hgrn_lb_logits'].astype(jnp.float32), axis=1), axis=1)
    for l in range(DEPTH):
        x = encoder_layer(x, mem, l, p, lb_all)
    return x


def setup_inputs(seed: int = 0) -> dict:
    key = jax.random.key(seed)
    ks = iter(jax.random.split(key, 40))
    nrm = lambda shape, scale: jax.random.normal(next(ks), shape, jnp.float32) * scale
    gain = lambda shape: 1.0 + nrm(shape, 0.05)
    return {
        'x_prompt': nrm((BATCH, SEQ, D_MODEL), 1.0),
        'x_sample': nrm((DEC_BATCH, DEC_SEQ, D_MODEL), 1.0),
        'mem_prompt': nrm((BATCH, N_MEM, D_MODEL), 1.0),
        'mem_sample': nrm((DEC_BATCH, N_MEM, D_MODEL), 1.0),
        'rel_bias': nrm((REL_BUCKETS, A_HEADS), 0.5),
        'hgrn_lb_logits': nrm((2, DEPTH + 1, B_WIDTH), 0.1),
        'g_pre_mix': gain((DEPTH, D_MODEL)),
        'w_in': nrm((DEPTH, D_MODEL, N_IN), D_MODEL ** -0.5),
        'b_merge': nrm((DEPTH, 2 * D_MODEL), 0.1),
        'lam_q1': nrm((DEPTH, A_HALF_DIM), 0.1),
        'lam_k1': nrm((DEPTH, A_HALF_DIM), 0.1),
        'lam_q2': nrm((DEPTH, A_HALF_DIM), 0.1),
        'lam_k2': nrm((DEPTH, A_HALF_DIM), 0.1),
        'g_subln': gain((DEPTH, A_VDIM)),
        'g_hgrn_norm': gain((DEPTH, B_DIM)),
        'w_branch_a': nrm((DEPTH, A_WIDTH, D_MODEL), A_WIDTH ** -0.5),
        'w_branch_b': nrm((DEPTH, B_WIDTH, D_MODEL), B_WIDTH ** -0.5),
        'w_out': nrm((DEPTH, D_MODEL, D_MODEL), D_MODEL ** -0.5),
        'g_post_mix': gain((DEPTH, D_MODEL)),
        'g_pre_x': gain((DEPTH, D_MODEL)),
        'g_mem': gain((DEPTH, D_MODEL)),
        'w_q_x': nrm((DEPTH, D_MODEL, X_WIDTH), D_MODEL ** -0.5),
        'w_kv_x': nrm((DEPTH, D_MODEL, 2 * X_WIDTH), D_MODEL ** -0.5),
        'w_o_x': nrm((DEPTH, X_WIDTH, D_MODEL), X_WIDTH ** -0.5),
        'g_post_x': gain((DEPTH, D_MODEL)),
        'g_pre_ffn': gain((DEPTH, D_MODEL)),
        'w_gate_up': nrm((DEPTH, D_MODEL, 2 * D_FF), D_MODEL ** -0.5),
        'w_down': nrm((DEPTH, D_FF, D_MODEL), D_FF ** -0.5),
        'g_post_ffn': gain((DEPTH, D_MODEL)),
    }


def reference(x_prompt, x_sample, mem_prompt, mem_sample, rel_bias, hgrn_lb_logits, g_pre_mix, w_in, b_merge,
              lam_q1, lam_k1, lam_q2, lam_k2, g_subln, g_hgrn_norm, w_branch_a, w_branch_b, w_out, g_post_mix,
              g_pre_x, g_mem, w_q_x, w_kv_x, w_o_x, g_post_x, g_pre_ffn, w_gate_up, w_down, g_post_ffn):
    p = dict(rel_bias=rel_bias, hgrn_lb_logits=hgrn_lb_logits, g_pre_mix=g_pre_mix, w_in=w_in, b_merge=b_merge,
             lam_q1=lam_q1, lam_k1=lam_k1, lam_q2=lam_q2, lam_k2=lam_k2, g_subln=g_subln,
             g_hgrn_norm=g_hgrn_norm, w_branch_a=w_branch_a, w_branch_b=w_branch_b, w_out=w_out,
             g_post_mix=g_post_mix, g_pre_x=g_pre_x, g_mem=g_mem, w_q_x=w_q_x, w_kv_x=w_kv_x, w_o_x=w_o_x,
             g_post_x=g_post_x, g_pre_ffn=g_pre_ffn, w_gate_up=w_gate_up, w_down=w_down, g_post_ffn=g_post_ffn)
    y_prompt = trunk(x_prompt, mem_prompt, p)
    y_sample = trunk(x_sample, mem_sample, p)
    return (y_prompt, y_sample)
```

```python
import math
from contextlib import ExitStack
import numpy as np
import ml_dtypes
import concourse.bass as bass
import concourse.mybir as mybir
from concourse.bass_utils import run_bass_kernel_spmd

F32 = mybir.dt.float32
BF16 = mybir.dt.bfloat16
AF = mybir.ActivationFunctionType
ALU = mybir.AluOpType
AX = mybir.AxisListType

D = 2048
KC = 16
NIN = 12288
DFF = 5632
NMEM = 256
EPS = 1e-6
LAM_INIT = 0.8 - 0.6 * math.exp(0.0)
PG = 8
OPT_ACC = True
OPT_PTF = True
OPT_PN = True
SAME_ENGINE_SYNC = True

WSPECS = [("w_in", 2048, 12288), ("w_ba", 1024, 2048), ("w_bb", 1024, 2048), ("w_out", 2048, 2048),
          ("w_q_x", 2048, 512), ("w_kv_x", 2048, 1024), ("w_o_x", 512, 2048),
          ("w_gu", 2048, 11264), ("w_down", 5632, 2048)]

C_GPM, C_GPX, C_GPF, C_GMEM, C_BM, C_LBL, C_GSUB, C_GHN, C_EPS, C_ZERO, C_NCOL = 0, 16, 32, 48, 64, 96, 128, 129, 130, 131, 132


class Tr:
    def __init__(self, nc, es):
        self.nc = nc
        self.es = es
        self.E = {"pe": nc.tensor, "act": nc.scalar, "dve": nc.vector, "pool": nc.gpsimd, "sp": nc.sync}
        self.ops = []

    def op(self, eng, fn, ak, r=(), w=()):
        self.ops.append(dict(k="c", eng=eng, fn=fn, ak=ak, r=tuple(r), w=tuple(w), need=False))

    def dma(self, q, out, in_, key, r=(), w=()):
        self.ops.append(dict(k="d", eng=q, out=out, in_=in_, key=key, r=tuple(r), w=tuple(w), need=True))

    def barrier(self):
        self.ops.append(dict(k="b"))

    def emit(self):
        ops = self.ops
        last_w, readers = {}, {}
        last_on_eng = {}
        for i, o in enumerate(ops):
            if o["k"] == "b":
                o["lasts"] = dict(last_on_eng)
                for j in last_on_eng.values():
                    ops[j]["need"] = True
                last_w.clear()
                readers.clear()
                continue
            deps = set()
            for t in o["r"]:
                if t in last_w:
                    deps.add(last_w[t])
            for t in o["w"]:
                if t in last_w:
                    deps.add(last_w[t])
                deps.update(readers.get(t, ()))
            deps.discard(i)
            o["deps"] = deps
            for t in o["r"]:
                readers.setdefault(t, []).append(i)
            for t in o["w"]:
                last_w[t] = i
                readers[t] = []
            for d in deps:
                ops[d]["need"] = True
            if o["k"] == "c":
                last_on_eng[o["eng"]] = i
        esem = {e: self.es.enter_context(self.nc.semaphore("se_" + e)) for e in ("pe", "act", "dve", "pool")}
        ecnt = {e: 0 for e in esem}
        ksem, kcnt = {}, {}
        waited = {e: {} for e in self.E}

        def wait(e, sem, name, val):
            if waited[e].get(name, 0) < val:
                self.E[e].wait_ge(sem, val)
                waited[e][name] = val

        for i, o in enumerate(ops):
            if o["k"] == "b":
                for e in self.E:
                    for f, j in o["lasts"].items():
                        if f != e:
                            wait(e, esem[f], "e" + f, ops[j]["sig"][2])
                    for kk, c in kcnt.items():
                        wait(e, ksem[kk], "k" + kk, c)
                continue
            e = o["eng"]
            for d in sorted(o["deps"]):
                p = ops[d]
                if p["k"] == "c" and o["k"] == "c" and p["eng"] == e and (e == "pe" or not SAME_ENGINE_SYNC):
                    continue
                sem, name, val = p["sig"]
                wait(e, sem, name, val)
            if o["k"] == "c":
                ins = o["fn"](*o["ak"][0], **o["ak"][1])
                if o["need"]:
                    ecnt[e] += 1
                    ins.then_inc(esem[e], 1)
                    o["sig"] = (esem[e], "e" + e, ecnt[e])
            else:
                key = o["key"]
                if key not in ksem:
                    ksem[key] = self.es.enter_context(self.nc.semaphore("sk_" + key))
                    kcnt[key] = 0
                kcnt[key] += 16
                self.E[e].dma_start(out=o["out"], in_=o["in_"]).then_inc(ksem[key], 16)
                o["sig"] = (ksem[key], "k" + key, kcnt[key])
        for f in esem:
            if ecnt[f]:
                wait("pool", esem[f], "e" + f, ecnt[f])
        for kk, c in kcnt.items():
            wait("pool", ksem[kk], "k" + kk, c)
        self.nsem = len(ksem) + 4


def KW(*a, **k):
    return (a, k)


class Rot:
    def __init__(self, name, bufs):
        self.name, self.bufs, self.i = name, bufs, 0

    def next(self):
        j = self.i % len(self.bufs)
        self.i += 1
        return self.bufs[j], (self.name, j)


def build(jobs):
    nc = bass.Bass("TRN2", target_bir_lowering=False)
    NLOC = sum(L for L, R in jobs)
    NREM = sum(R for L, R in jobs)
    NJ = len(jobs)
    TMAX = max(L + R for L, R in jobs)
    LMAX = max(L for L, R in jobs)

    def din(name, shape, dt=F32):
        return nc.dram_tensor(name, list(shape), dt, kind="ExternalInput").ap()

    def dscr(name, shape, dt):
        return nc.dram_tensor(name, list(shape), dt, kind="Internal").ap()

    xs = din("xs", [NLOC + NREM, D])
    mems = din("mems", [NJ * NMEM, D])
    wsrc = {n: din(n, [K, N]) for n, K, N in WSPECS}
    cstf = din("cst_f32", [128, C_NCOL])
    rows = din("rows_f32", [3, D])
    lamv = din("lamv", [1, 256])
    relb = din("rel_bias", [32, 8])
    e1h = din("e1h", [32, 1280])
    cmaskd = din("cmask", [128, 2048])
    cstb = din("cst_bf", [128, 512], BF16)
    y = nc.dram_tensor("y", [NLOC, D], F32, kind="ExternalOutput").ap()

    wpan = {}
    for n, K, N in WSPECS:
        nkg = -(-(K // 128) // PG)
        wpan[n] = dscr("wb_" + n, [N // 512, nkg, 128, PG, 512], BF16)
    s_qa = dscr("s_qa", [8, 128, LMAX], BF16)
    s_ka = dscr("s_ka", [8, 128, TMAX], BF16)
    s_va = dscr("s_va", [8, 128, TMAX // 128, 128], BF16)
    s_qb = dscr("s_qb", [8, 128, LMAX], BF16)
    s_zf = dscr("s_zf", [8, 128, LMAX], F32)
    s_zb = dscr("s_zb", [8, 128, TMAX], F32)
    s_vb = dscr("s_vb", [8, 128, TMAX // 128, 128], BF16)
    s_sg = dscr("s_sg", [8, 128, LMAX], BF16)
    s_gt = dscr("s_gt", [32, 128, LMAX], BF16)
    s_ya = dscr("s_ya", [8, 128, LMAX], BF16)
    s_yb = dscr("s_yb", [8, 128, LMAX], BF16)
    s_tb0 = dscr("s_tb0", [8, 1280], F32)
    s_rep = dscr("s_rep", [8, 129, 1280], F32)

    es = ExitStack()
    with es:
        T = Tr(nc, es)

        _uid = [0]

        def sb(st, name, shape, dt):
            _uid[0] += 1
            return st.enter_context(nc.sbuf_tensor("%s_%d" % (name, _uid[0]), list(shape), dt))

        CF = sb(es, "CF", [128, C_NCOL], F32)
        CB = sb(es, "CB", [128, 512], BF16)
        LBV = sb(es, "LBV", [128, 32], F32)
        LAMT = sb(es, "LAMT", [128, 8], F32)
        FARB = sb(es, "FARB", [128, 16], F32)
        ONE1 = sb(es, "ONE1", [128, 1], F32)
        IDB = CB[:, 0:128]
        ONES = CB[:, 128:256]
        MSKF = CB[:, 256:384]
        MSKB = CB[:, 384:512]
        EPSC = CF[:, C_EPS:C_EPS + 1]
        PT = [es.enter_context(nc.psum_tensor("PT%d" % i, [128, 1024], BF16)) for i in range(2)]
        PS = [es.enter_context(nc.psum_tensor("PS%d" % i, [128, 512], F32)) for i in range(6)]

        T.dma("sp", CF[:], cstf[:, :], "CF", w=["CF"])
        T.dma("sp", CB[:], cstb[:, :], "CB", w=["CB"])
        T.op("dve", nc.vector.memset, KW(ONE1[:], 1.0), w=["ONE1"])
        for n, K, N in WSPECS:
            nkc = K // 128
            for cp in range(N // 512):
                for kg in range(-(-nkc // PG)):
                    g = min(PG, nkc - kg * PG)
                    src = bass.AP(tensor=wsrc[n].tensor, offset=kg * PG * 128 * N + cp * 512,
                                  ap=[[N, 128], [128 * N, g], [1, 512]])
                    T.dma("pool", wpan[n][cp, kg, :, 0:g, :], src, "cv", w=[("wb", n, cp, kg)])
        with ExitStack() as ph:
            LQ = sb(ph, "LQ", [128, 256], F32)
            LP = sb(ph, "LP", [128, 128], F32)
            LS = sb(ph, "LS", [128, 4], F32)
            RB = sb(ph, "RB", [32, 8], F32)
            EH = sb(ph, "EH", [32, 1280], F32)
            TB = sb(ph, "TB", [8, 1280], F32)
            T.dma("sp", LQ[:], lamv.partition_broadcast(128) if False else bass.AP(tensor=lamv.tensor, offset=0, ap=[[0, 128], [1, 256]]), "LQ", w=["LQ"])
            T.dma("sp", RB[:], relb[:, :], "RB", w=["RB"])
            T.dma("sp", EH[:], e1h[:, :], "EH", w=["EH"])
            T.dma("sp", FARB[:, 0:8], bass.AP(tensor=relb.tensor, offset=15 * 8, ap=[[0, 128], [1, 8]]), "FARB", w=["FARB"])
            T.dma("sp", FARB[:, 8:16], bass.AP(tensor=relb.tensor, offset=31 * 8, ap=[[0, 128], [1, 8]]), "FARB", w=["FARB"])
            T.op("dve", nc.vector.tensor_tensor, KW(out=LP[:, 0:64], in0=LQ[:, 0:64], in1=LQ[:, 64:128], op=ALU.mult), r=["LQ"], w=["LP"])
            T.op("dve", nc.vector.tensor_tensor, KW(out=LP[:, 64:128], in0=LQ[:, 128:192], in1=LQ[:, 192:256], op=ALU.mult), r=["LQ"], w=["LP"])
            T.op("dve", nc.vector.tensor_reduce, KW(out=LS[:, 0:1], in_=LP[:, 0:64], axis=AX.X, op=ALU.add), r=["LP"], w=["LS"])
            T.op("dve", nc.vector.tensor_reduce, KW(out=LS[:, 1:2], in_=LP[:, 64:128], axis=AX.X, op=ALU.add), r=["LP"], w=["LS"])
            T.op("act", nc.scalar.activation, KW(out=LS[:, 2:4], in_=LS[:, 0:2], func=AF.Exp), r=["LS"], w=["LS"])
            T.op("dve", nc.vector.tensor_tensor, KW(out=LAMT[:, 0:1], in0=LS[:, 2:3], in1=LS[:, 3:4], op=ALU.subtract), r=["LS"], w=["LAMT"])
            T.op("dve", nc.vector.tensor_scalar, KW(out=LAMT[:, 0:1], in0=LAMT[:, 0:1], scalar1=LAM_INIT, scalar2=None, op0=ALU.add), r=["LAMT"], w=["LAMT"])
            T.op("dve", nc.vector.tensor_scalar, KW(out=LAMT[:, 1:2], in0=LAMT[:, 0:1], scalar1=-1.0, scalar2=None, op0=ALU.mult), r=["LAMT"], w=["LAMT"])
            T.op("dve", nc.vector.tensor_scalar, KW(out=LAMT[:, 2:3], in0=CF[:, C_GSUB:C_GSUB + 1], scalar1=1.0 - LAM_INIT, scalar2=None, op0=ALU.mult), r=["CF", "LAMT"], w=["LAMT"])
            lbl = CF[:, C_LBL:C_LBL + 32].rearrange("p (a s h) -> p a s h", a=2, s=2)
            T.op("dve", nc.vector.tensor_tensor, KW(out=LBV[:, 0:16].rearrange("p (a h) -> p a h", a=2), in0=lbl[:, :, 0, :], in1=lbl[:, :, 1, :], op=ALU.subtract), r=["CF"], w=["LBV"])
            T.op("act", nc.scalar.activation, KW(out=LBV[:, 0:16], in_=LBV[:, 0:16], func=AF.Sigmoid), r=["LBV"], w=["LBV"])
            T.op("dve", nc.vector.tensor_scalar, KW(out=LBV[:, 16:32], in0=LBV[:, 0:16], scalar1=-1.0, scalar2=1.0, op0=ALU.mult, op1=ALU.add), r=["LBV"], w=["LBV"])
            for pc in range(3):
                w_ = 512 if pc < 2 else 256
                T.op("pe", nc.tensor.matmul, KW(PS[0][0:8, 0:w_], lhsT=RB[:, :], rhs=EH[:, pc * 512:pc * 512 + w_], start=True, stop=True), r=["RB", "EH"], w=["ps0"])
                T.op("dve", nc.vector.tensor_copy, KW(out=TB[:, pc * 512:pc * 512 + w_], in_=PS[0][0:8, 0:w_]), r=["ps0"], w=["TB"])
            T.dma("sp", s_tb0[:, :], TB[:], "TB", r=["TB"], w=["tb0"])
            T.dma("sp", s_rep[:, :, :], bass.AP(tensor=s_tb0.tensor, offset=0, ap=[[1280, 8], [0, 129], [1, 1280]]), "REP", r=["tb0"], w=["rep"])
            T.barrier()

        def norm_T(ph_bufs, src_ap, src_tok, gcol, dst, dst_tok, c0, rtok=()):
            SS, XN = ph_bufs
            ss, sst = SS.next()
            xn, xnt = XN.next()
            if OPT_ACC:
                T.op("act", nc.scalar.activation, KW(out=xn[:], in_=src_ap, func=AF.Square, accum_out=ss[:, 0:1]), r=[src_tok, *rtok], w=[xnt, sst])
            else:
                T.op("act", nc.scalar.activation, KW(out=xn[:], in_=src_ap, func=AF.Square), r=[src_tok, *rtok], w=[xnt])
                T.op("dve", nc.vector.tensor_reduce, KW(out=ss[:, 0:1], in_=xn[:], axis=AX.X, op=ALU.add), r=[xnt], w=[sst])
            T.op("act", nc.scalar.activation, KW(out=ss[:, 1:2], in_=ss[:, 0:1], func=AF.Ln, bias=EPSC, scale=1.0 / D), r=[sst], w=[sst])
            T.op("act", nc.scalar.activation, KW(out=ss[:, 2:3], in_=ss[:, 1:2], func=AF.Exp, scale=-0.5), r=[sst], w=[sst])
            T.op("act", nc.scalar.activation, KW(out=xn[:], in_=src_ap, func=AF.Copy, scale=ss[:, 2:3]), r=[src_tok, sst], w=[xnt])
            for q4 in range(4):
                reg = PTR.i % 2
                PTR.i += 1
                ptt = ("PT", reg)
                for j in range(4):
                    kc = q4 * 4 + j
                    T.op("pe", nc.tensor.transpose, KW(out=PT[reg][:, j * 128:(j + 1) * 128], in_=xn[:, kc * 128:(kc + 1) * 128], identity=IDB), r=[xnt], w=[ptt])
                gv = CF[:, gcol + q4 * 4: gcol + q4 * 4 + 4]
                T.op("dve", nc.vector.tensor_tensor, KW(
                    out=dst[:, q4 * 4:(q4 + 1) * 4, c0:c0 + 128],
                    in0=PT[reg][:, 0:512].rearrange("p (a b) -> p a b", a=4),
                    in1=gv.unsqueeze(2).broadcast_to([128, 4, 128]), op=ALU.mult), r=[ptt], w=[dst_tok])
            return ss

        class _C:
            pass
        PTR = _C()
        PTR.i = 0

        def load_panel(WP, name, cp, kg, g_n=PG):
            wb, wt = WP.next()
            T.dma("sp", wb[:, 0:g_n, :], wpan[name][cp, kg, :, 0:g_n, :], "wp%d" % wt[1], w=[wt])
            return wb, wt

        lo = 0
        ro = NLOC
        for ji, (L, R) in enumerate(jobs):
            TT = L + R
            NB = TT // 128
            with ExitStack() as ph:
                XS = Rot("XS", [sb(ph, "p1xs%d" % i, [128, D], F32) for i in range(2)])
                SS = Rot("SS", [sb(ph, "p1ss%d" % i, [128, 4], F32) for i in range(2)])
                XN = Rot("XN", [sb(ph, "p1xn%d" % i, [128, D], BF16) for i in range(2)])
                HT = Rot("HT", [sb(ph, "p1ht%d" % i, [128, KC, 512], BF16) for i in range(2)])
                WP = Rot("WP", [sb(ph, "p1wp%d" % i, [128, PG, 512], BF16) for i in range(4)])
                STB = Rot("STB", [sb(ph, "p1sb%d" % i, [128, 512], BF16) for i in range(6)])
                STF = Rot("STF", [sb(ph, "p1sf%d" % i, [128, 512], F32) for i in range(4)])
                PSR = Rot("PS", PS)
                tiles = [(lo + t * 512, t * 512, True) for t in range(L // 512)] + \
                        [(ro + t * 512, L + t * 512, False) for t in range(R // 512)]
                def p1_norm(row0):
                    ht, htt = HT.next()
                    for sub in range(4):
                        xb, xt = XS.next()
                        T.dma("pool", xb[:], xs[row0 + sub * 128: row0 + (sub + 1) * 128, :], "xs%d" % xt[1], w=[xt])
                        norm_T((SS, XN), xb[:], xt, C_GPM, ht, htt, sub * 128)
                    return ht, htt
                nxt = p1_norm(tiles[0][0])
                for ti, (row0, pos0, is_loc) in enumerate(tiles):
                    ht, htt = nxt
                    if ti + 1 < len(tiles):
                        nxt = p1_norm(tiles[ti + 1][0])
                    cps = list(range(24)) if is_loc else [2, 3, 4, 5, 10, 11, 12, 13]
                    for cp in cps:
                        cat = cp // 2 if cp < 16 else 8
                        tokmajor = cat in (2, 6)
                        banks = [PSR.next() for _ in range(4)]
                        for kg in range(2):
                            wb, wt = load_panel(WP, "w_in", cp, kg)
                            for j in range(4):
                                pb, pbt = banks[j]
                                for g in range(PG):
                                    kc = kg * PG + g
                                    st, sp_ = (kc == 0), (kc == KC - 1)
                                    if tokmajor:
                                        T.op("pe", nc.tensor.matmul, KW(pb[:, :], lhsT=ht[:, kc, j * 128:(j + 1) * 128], rhs=wb[:, g, :], start=st, stop=sp_), r=[htt, wt], w=[pbt])
                                    else:
                                        T.op("pe", nc.tensor.matmul, KW(pb[:, :], lhsT=wb[:, g, j * 128:(j + 1) * 128], rhs=ht[:, kc, :], start=st, stop=sp_), r=[htt, wt], w=[pbt])
                        for j in range(4):
                            pb, pbt = banks[j]
                            ch = (cp % 2) * 4 + j
                            if cat in (4, 5):
                                stg, stt = STF.next()
                            else:
                                stg, stt = STB.next()
                            key = "st%s%d" % (stt[0], stt[1])
                            if cat == 0:
                                T.op("dve", nc.vector.tensor_scalar, KW(out=stg[:], in0=pb[:, :], scalar1=0.125, scalar2=None, op0=ALU.mult), r=[pbt], w=[stt])
                                T.dma("pool", s_qa[ch, :, pos0:pos0 + 512], stg[:], key, r=[stt], w=[("qa", ch)])
                            elif cat == 1:
                                T.op("dve", nc.vector.tensor_copy, KW(out=stg[:], in_=pb[:, :]), r=[pbt], w=[stt])
                                T.dma("pool", s_ka[ch, :, pos0:pos0 + 512], stg[:], key, r=[stt], w=[("ka", ch)])
                            elif cat in (2, 6):
                                dst = s_va if cat == 2 else s_vb
                                h0 = (cp % 2) * 4
                                blk = pos0 // 128 + j
                                T.op("act", nc.scalar.copy, KW(out=stg[:], in_=pb[:, :]), r=[pbt], w=[stt])
                                d_ap = dst[h0:h0 + 4, :, blk, :].rearrange("h p e -> p h e")
                                T.dma("pool", d_ap, stg[:].rearrange("p (h e) -> p h e", h=4), key, r=[stt], w=[("v", cat, h0)])
                            elif cat == 3:
                                T.op("dve", nc.vector.tensor_scalar, KW(out=stg[:], in0=pb[:, :], scalar1=128 ** -0.5, scalar2=None, op0=ALU.mult), r=[pbt], w=[stt])
                                T.dma("pool", s_qb[ch, :, pos0:pos0 + 512], stg[:], key, r=[stt], w=[("qb", ch)])
                            elif cat == 4:
                                T.op("act", nc.scalar.copy, KW(out=stg[:], in_=pb[:, :]), r=[pbt], w=[stt])
                                T.dma("pool", s_zf[ch, :, pos0:pos0 + 512], stg[:], key, r=[stt], w=[("zf", ch)])
                            elif cat == 5:
                                T.op("dve", nc.vector.tensor_copy, KW(out=stg[:], in_=pb[:, :]), r=[pbt], w=[stt])
                                T.dma("pool", s_zb[ch, :, pos0:pos0 + 512], stg[:], key, r=[stt], w=[("zb", ch)])
                            elif cat == 7:
                                T.op("act", nc.scalar.activation, KW(out=stg[:], in_=pb[:, :], func=AF.Silu), r=[pbt], w=[stt])
                                T.dma("pool", s_sg[ch, :, pos0:pos0 + 512], stg[:], key, r=[stt], w=[("sg", ch)])
                            else:
                                gc = (cp - 16) * 4 + j
                                T.op("act", nc.scalar.activation, KW(out=stg[:], in_=pb[:, :], func=AF.Sigmoid, bias=CF[:, C_BM + gc:C_BM + gc + 1]), r=[pbt], w=[stt])
                                T.dma("pool", s_gt[gc, :, pos0:pos0 + 512], stg[:], key, r=[stt], w=[("gt", gc)])
                T.barrier()

            with ExitStack() as ph:
                KT = Rot("KT", [sb(ph, "p2kt%d" % i, [128, TT], BF16) for i in range(2)])
                VV = Rot("VV", [sb(ph, "p2vv%d" % i, [128, NB, 128], BF16) for i in range(2)])
                QT = Rot("QT", [sb(ph, "p2qt%d" % i, [128, L], BF16) for i in range(2)])
                BT = Rot("BT", [sb(ph, "p2bt%d" % i, [128, 6, 512], F32) for i in range(2)])
                PP = Rot("PP", [sb(ph, "p2pp%d" % i, [128, 512], BF16) for i in range(6)])
                SBT = Rot("SBT", [sb(ph, "p2sbt%d" % i, [128, 512], F32) for i in range(3)])
                FZ = [sb(ph, "p2fz%d" % i, [128, 512], F32) for i in range(5)]
                ACC = [sb(ph, "p2acc%d" % i, [128, 512], F32) for i in range(2)]
                ONESF = sb(ph, "p2onesf", [128, 128], F32)
                T.op("pool", nc.gpsimd.memset, KW(ONESF[:], 1.0), w=["ONESF"])
                SQ = sb(ph, "p2sq", [128, 512], BF16)
                YS = Rot("YS", [sb(ph, "p2ys%d" % i, [128, 512], BF16) for i in range(2)])
                if OPT_PTF:
                    SR = Rot("SR", [PS[0], PS[1], PT[0][:, :].bitcast(F32), PT[1][:, :].bitcast(F32)])
                    LA = 4
                else:
                    SR = Rot("SR", [PS[0], PS[1]])
                    LA = 2
                for h in range(8):
                    kt, ktt = KT.next()
                    vv, vvt = VV.next()
                    qt, qtt = QT.next()
                    bt, btt = BT.next()
                    T.dma("sp", kt[:], s_ka[h, :, 0:TT], "kt%d" % ktt[1], w=[ktt])
                    T.dma("sp", vv[:], s_va[h, :, 0:NB, :], "vv%d" % vvt[1], w=[vvt])
                    T.dma("sp", qt[:], s_qa[h, :, 0:L], "qt%d" % qtt[1], w=[qtt])
                    for oi in range(6):
                        off = -128 + oi * 128
                        T.dma("sp", bt[:, oi, :], bass.AP(tensor=s_rep.tensor, offset=h * 129 * 1280 + 640 - off, ap=[[1279, 128], [1, 512]]), "bt%d" % btt[1], w=[btt])
                    for qb in range(L // 512):
                        q0 = qb * 512
                        steps = [(c, kb) for kb in range(NB) for c in range(2)]
                        pend = {}

                        def qk_step(i, q0=q0, kt=kt, qt=qt, bt=bt, h=h, pend=pend, ktt=ktt, qtt=qtt, btt=btt):
                            c, kb = steps[i]
                            off = kb * 128 - q0
                            sbk, sbt_ = SR.next()
                            pp, ppt = PP.next()
                            T.op("pe", nc.tensor.matmul, KW(sbk[:, :], lhsT=kt[c * 64:(c + 1) * 64, kb * 128:(kb + 1) * 128], rhs=qt[c * 64:(c + 1) * 64, q0:q0 + 512], start=True, stop=True), r=[ktt, qtt], w=[sbt_])
                            if -128 <= off <= 512:
                                oi = (off + 128) // 128
                                tb_, tbt = SBT.next()
                                T.op("dve", nc.vector.tensor_tensor, KW(out=tb_[:], in0=sbk[:, :], in1=bt[:, oi, :], op=ALU.add), r=[sbt_, btt], w=[tbt])
                                T.op("act", nc.scalar.activation, KW(out=pp[:], in_=tb_[:], func=AF.Exp), r=[tbt], w=[ppt])
                            else:
                                fcol = h if off < 0 else 8 + h
                                T.op("act", nc.scalar.activation, KW(out=pp[:], in_=sbk[:, :], func=AF.Exp, bias=FARB[:, fcol:fcol + 1]), r=[sbt_], w=[ppt])
                            if kb == 0:
                                T.op("dve", nc.vector.tensor_copy, KW(out=ACC[c][:], in_=pp[:]), r=[ppt], w=[("acc", c)])
                            else:
                                T.op("dve", nc.vector.tensor_tensor, KW(out=ACC[c][:], in0=ACC[c][:], in1=pp[:], op=ALU.add), r=[ppt, ("acc", c)], w=[("acc", c)])
                            pend[i] = (pp, ppt)

                        def pv_step(i, vv=vv, vvt=vvt, pend=pend):
                            c, kb = steps[i]
                            pp, ppt = pend.pop(i)
                            st, sp_ = (kb == 0), (kb == NB - 1)
                            T.op("pe", nc.tensor.matmul, KW(PS[2 + c][:, :], lhsT=vv[:, kb, :], rhs=pp[:], start=st, stop=sp_), r=[vvt, ppt], w=[("psO", c)])
                        for i in range(min(LA, len(steps))):
                            qk_step(i)
                        for i in range(0, len(steps), 2):
                            pv_step(i)
                            pv_step(i + 1)
                            for k2 in (i + LA, i + LA + 1):
                                if k2 < len(steps):
                                    qk_step(k2)
                        for c in range(2):
                            T.op("pe", nc.tensor.matmul, KW(PS[4 + c][:, :], lhsT=ONESF[:], rhs=ACC[c][:], start=True, stop=True), r=[("acc", c), "ONESF"], w=[("psZ", c)])
                        T.op("dve", nc.vector.reciprocal, KW(out=FZ[0][:], in_=PS[4][:, :]), r=[("psZ", 0)], w=["fz0"])
                        T.op("dve", nc.vector.reciprocal, KW(out=FZ[1][:], in_=PS[5][:, :]), r=[("psZ", 1)], w=["fz1"])
                        T.op("dve", nc.vector.tensor_tensor, KW(out=FZ[2][:], in0=PS[2][:, :], in1=FZ[0][:], op=ALU.mult), r=[("psO", 0), "fz0"], w=["fz2"])
                        T.op("dve", nc.vector.tensor_tensor, KW(out=FZ[3][:], in0=PS[3][:, :], in1=FZ[1][:], op=ALU.mult), r=[("psO", 1), "fz1"], w=["fz3"])
                        T.op("dve", nc.vector.scalar_tensor_tensor, KW(out=FZ[4][:], in0=FZ[3][:], scalar=LAMT[:, 1:2], in1=FZ[2][:], op0=ALU.mult, op1=ALU.add), r=["fz2", "fz3"], w=["fz4"])
                        T.op("act", nc.scalar.activation, KW(out=SQ[:], in_=FZ[4][:], func=AF.Square), r=["fz4"], w=["sq"])
                        mbk, mbt = SR.next()
                        T.op("pe", nc.tensor.matmul, KW(mbk[:, :], lhsT=ONES, rhs=SQ[:], start=True, stop=True), r=["sq"], w=[mbt])
                        T.op("act", nc.scalar.activation, KW(out=FZ[0][:], in_=mbk[:, :], func=AF.Ln, bias=EPSC, scale=1.0 / 128), r=[mbt], w=["fz0"])
                        T.op("act", nc.scalar.activation, KW(out=FZ[0][:], in_=FZ[0][:], func=AF.Exp, scale=-0.5), r=["fz0"], w=["fz0"])
                        ys, yst = YS.next()
                        T.op("dve", nc.vector.scalar_tensor_tensor, KW(out=ys[:], in0=FZ[4][:], scalar=LAMT[:, 2:3], in1=FZ[0][:], op0=ALU.mult, op1=ALU.mult), r=["fz4", "fz0"], w=[yst])
                        T.dma("pool", s_ya[h, :, q0:q0 + 512], ys[:], "ys%d" % yst[1], r=[yst], w=[("ya", h)])
                T.barrier()

            with ExitStack() as ph:
                PW = min(2048, L)
                NCH = PW // 64
                NBK = PW // 128
                CM = sb(ph, "p3cm", [128, PW], F32)
                A = sb(ph, "p3A", [128, PW], F32)
                KK = sb(ph, "p3KK", [128, PW], F32)
                C = sb(ph, "p3C", [128, PW], F32)
                E4 = sb(ph, "p3E4", [128, PW], F32)
                D1 = sb(ph, "p3D1", [128, PW], F32)
                EX = sb(ph, "p3EX", [128, PW], F32)
                DEC = sb(ph, "p3DEC", [128, NCH], F32)
                Q = sb(ph, "p3Q", [128, PW], BF16)
                SG = sb(ph, "p3SG", [128, PW], BF16)
                QTt = sb(ph, "p3QT", [128, PW], BF16)
                KTt = sb(ph, "p3KT", [128, PW], BF16)
                KH = sb(ph, "p3KH", [128, PW], BF16)
                QE = sb(ph, "p3QE", [128, PW], BF16)
                QT2 = sb(ph, "p3QT2", [128, PW], BF16)
                KT2 = sb(ph, "p3KT2", [128, PW], BF16)
                QE2 = sb(ph, "p3QE2", [128, PW], BF16)
                VB = sb(ph, "p3VB", [128, NBK, 128], BF16)
                SPV = sb(ph, "p3SPV", [128, L // 64, 128], BF16)
                SF = Rot("SF", [sb(ph, "p3sf%d" % i, [128, 128], F32) for i in range(2)])
                SFA = sb(ph, "p3SFA", [128, NCH, 128], BF16)
                AMA = sb(ph, "p3AMA", [128, NBK, 2, 128], BF16)
                KHT = Rot("KHT", [sb(ph, "p3kht%d" % i, [128, 128], BF16) for i in range(3)])
                OS = sb(ph, "p3OS", [128, 512], F32)
                RS = sb(ph, "p3RS", [128, 512], F32)
                Y1 = sb(ph, "p3Y1", [128, 512], F32)
                SQ = sb(ph, "p3sq", [128, 512], BF16)
                YS = Rot("YS2", [sb(ph, "p3ys%d" % i, [128, 512], BF16) for i in range(2)])
                T.dma("sp", CM[:], cmaskd[:, 0:PW], "CM", w=["CM"])
                NH = 2 if PW >= 1024 else 1
                HW_ = PW // NH
                NCH2 = HW_ // 64
                hs = lambda t, hf: t[:, hf * HW_:(hf + 1) * HW_]
                c3 = lambda t, hf: t[:, hf * HW_:(hf + 1) * HW_].rearrange("p (c j) -> p c j", j=64)
                hb_of = lambda b: (b * 128) // HW_

                def halves(fn):
                    a0 = len(T.ops)
                    fn(0)
                    if NH == 1:
                        return
                    a1 = len(T.ops)
                    fn(1)
                    a2 = len(T.ops)
                    assert a1 - a0 == a2 - a1
                    A_, B_ = T.ops[a0:a1], T.ops[a1:a2]
                    T.ops[a0:a2] = [x for pair in zip(A_, B_) for x in pair]

                def gate_math(h, d, zsrc, p0, hf):
                    Ah, tA, tK, tC = hs(A, hf), ("A", hf), ("KK", hf), ("C", hf)
                    T.dma("sp", Ah, zsrc[h, :, p0 + hf * HW_:p0 + (hf + 1) * HW_], "A%d" % hf, w=[tA])
                    T.op("act", nc.scalar.activation, KW(out=Ah, in_=Ah, func=AF.Sigmoid, scale=-1.0), r=[tA], w=[tA])
                    T.op("act", nc.scalar.activation, KW(out=hs(KK, hf), in_=Ah, func=AF.Copy, scale=LBV[:, 16 + d * 8 + h:17 + d * 8 + h]), r=[tA], w=[tK])
                    T.op("act", nc.scalar.activation, KW(out=Ah, in_=hs(KK, hf), func=AF.Ln, bias=ONE1[:, 0:1], scale=-1.0), r=[tK], w=[tA])
                    T.op("dve", nc.vector.tensor_tensor_scan, KW(out=hs(C, hf), data0=hs(CM, hf), data1=Ah, initial=0.0, op0=ALU.mult, op1=ALU.add), r=[tA, "CM"], w=[tC])

                def ew_sweepA(h, p0, hf):
                    gate_math(h, 1, s_zb, p0, hf)
                    tA, tK, tC, tE = ("A", hf), ("KK", hf), ("C", hf), ("EX", hf)
                    T.op("dve", nc.vector.tensor_tensor, KW(out=hs(EX, hf), in0=hs(C, hf), in1=hs(A, hf), op=ALU.subtract), r=[tC, tA], w=[tE])
                    T.op("act", nc.scalar.activation, KW(out=hs(EX, hf), in_=hs(EX, hf), func=AF.Exp), r=[tE], w=[tE])
                    T.op("dve", nc.vector.tensor_tensor, KW(out=hs(KH, hf), in0=hs(KK, hf), in1=hs(EX, hf), op=ALU.mult), r=[tK, tE], w=[("KH", hf)])
                    T.op("act", nc.scalar.activation, KW(out=DEC[:, hf * NCH2:(hf + 1) * NCH2], in_=c3(C, hf)[:, :, 63], func=AF.Exp), r=[tC], w=[("DEC", hf)])

                def ew_sweepB(h, p0, hf):
                    tA, tK, tC, tE, tD, t4 = ("A", hf), ("KK", hf), ("C", hf), ("EX", hf), ("D1", hf), ("E4", hf)
                    tQ = ("Q", hf)
                    Qh, KKh, E4h, D1h = hs(Q, hf), hs(KK, hf), hs(E4, hf), hs(D1, hf)
                    bc = lambda t, col: c3(t, hf)[:, :, col:col + 1].broadcast_to([128, NCH2, 64])
                    T.dma("sp", Qh, s_qb[h, :, p0 + hf * HW_:p0 + (hf + 1) * HW_], "Q%d" % hf, w=[tQ])
                    T.dma("sp", hs(SG, hf), s_sg[h, :, p0 + hf * HW_:p0 + (hf + 1) * HW_], "SG%d" % hf, w=[("SG", hf)])
                    gate_math(h, 1, s_zb, p0, hf)
                    T.op("dve", nc.vector.tensor_tensor, KW(out=hs(EX, hf), in0=hs(C, hf), in1=hs(A, hf), op=ALU.subtract), r=[tC, tA], w=[tE])
                    T.op("dve", nc.vector.tensor_tensor, KW(out=c3(D1, hf), in0=c3(EX, hf), in1=bc(C, 31), op=ALU.subtract), r=[tE, tC], w=[tD])
                    T.op("act", nc.scalar.activation, KW(out=E4h, in_=D1h, func=AF.Exp), r=[tD], w=[t4])
                    T.op("dve", nc.vector.tensor_tensor, KW(out=hs(KT2, hf), in0=KKh, in1=E4h, op=ALU.mult), r=[tK, t4], w=[("KT2", hf)])
                    T.op("act", nc.scalar.activation, KW(out=E4h, in_=D1h, func=AF.Exp, scale=-1.0), r=[tD], w=[t4])
                    T.op("dve", nc.vector.tensor_tensor, KW(out=hs(QT2, hf), in0=Qh, in1=E4h, op=ALU.mult), r=[tQ, t4], w=[("QT2", hf)])
                    T.op("dve", nc.vector.tensor_tensor, KW(out=c3(D1, hf), in0=c3(EX, hf), in1=bc(C, 63), op=ALU.subtract), r=[tE, tC], w=[tD])
                    T.op("act", nc.scalar.activation, KW(out=E4h, in_=D1h, func=AF.Exp, scale=-1.0), r=[tD], w=[t4])
                    T.op("dve", nc.vector.tensor_tensor, KW(out=hs(QE2, hf), in0=Qh, in1=E4h, op=ALU.mult), r=[tQ, t4], w=[("QE2", hf)])
                    gate_math(h, 0, s_zf, p0, hf)
                    T.op("dve", nc.vector.tensor_tensor, KW(out=c3(D1, hf), in0=c3(C, hf), in1=bc(C, 31), op=ALU.subtract), r=[tC], w=[tD])
                    T.op("act", nc.scalar.activation, KW(out=E4h, in_=D1h, func=AF.Exp), r=[tD], w=[t4])
                    T.op("dve", nc.vector.tensor_tensor, KW(out=hs(QTt, hf), in0=Qh, in1=E4h, op=ALU.mult), r=[tQ, t4], w=[("QT", hf)])
                    T.op("act", nc.scalar.activation, KW(out=E4h, in_=D1h, func=AF.Exp, scale=-1.0), r=[tD], w=[t4])
                    T.op("dve", nc.vector.tensor_tensor, KW(out=hs(KTt, hf), in0=KKh, in1=E4h, op=ALU.mult), r=[tK, t4], w=[("KT", hf)])
                    T.op("dve", nc.vector.tensor_tensor, KW(out=c3(D1, hf), in0=c3(C, hf), in1=bc(C, 63), op=ALU.subtract), r=[tC], w=[tD])
                    T.op("act", nc.scalar.activation, KW(out=E4h, in_=D1h, func=AF.Exp, scale=-1.0), r=[tD], w=[t4])
                    T.op("dve", nc.vector.tensor_tensor, KW(out=hs(KH, hf), in0=KKh, in1=E4h, op=ALU.mult), r=[tK, t4], w=[("KH", hf)])
                    T.op("act", nc.scalar.activation, KW(out=E4h, in_=hs(C, hf), func=AF.Exp), r=[tC], w=[t4])
                    T.op("dve", nc.vector.tensor_tensor, KW(out=hs(QE, hf), in0=Qh, in1=E4h, op=ALU.mult), r=[tQ, t4], w=[("QE", hf)])
                    T.op("dve", nc.vector.tensor_copy, KW(out=DEC[:, hf * NCH2:(hf + 1) * NCH2], in_=c3(E4, hf)[:, :, 63]), r=[t4], w=[("DEC", hf)])

                def khT_block(b):
                    reg = PTR.i % 2
                    PTR.i += 1
                    ptt = ("PT", reg)
                    kh, kht = KHT.next()
                    T.op("pe", nc.tensor.transpose, KW(out=PT[reg][:, 0:128], in_=KH[:, b * 128:(b + 1) * 128], identity=IDB), r=[("KH", hb_of(b))], w=[ptt])
                    T.op("act", nc.scalar.copy, KW(out=kh[:], in_=PT[reg][:, 0:128]), r=[ptt], w=[kht])
                    return kh, kht

                KVR = _C()
                KVR.i = 0

                def state_step(kh, kht, b, part, chunk_local, scur, scurt):
                    reg = KVR.i % 2
                    KVR.i += 1
                    kvt = ("kv", reg)
                    p0_, p1_ = part * 64, part * 64 + 64
                    T.op("pe", nc.tensor.matmul, KW(PS[reg][:, 0:128], lhsT=kh[p0_:p1_, :], rhs=VB[p0_:p1_, b, :], start=True, stop=True), r=[kht, "VB"], w=[kvt])
                    snew, snewt = SF.next()
                    T.op("dve", nc.vector.scalar_tensor_tensor, KW(out=snew[:], in0=scur[:], scalar=DEC[:, chunk_local:chunk_local + 1], in1=PS[reg][:, 0:128], op0=ALU.mult, op1=ALU.add), r=[scurt, ("DEC", hb_of(b)), kvt], w=[snewt])
                    return snew, snewt

                for h in range(8):
                    scur, scurt = SF.next()
                    T.op("pool", nc.gpsimd.memset, KW(scur[:], 0.0), w=[scurt])
                    if R == 0:
                        T.op("pool", nc.gpsimd.memset, KW(SPV[:, L // 64 - 1, :], 0.0), w=[("SPV", L // 64 - 1)])
                    for pi in reversed(range(TT // PW)):
                        p0 = pi * PW
                        T.dma("sp", VB[:], s_vb[h, :, p0 // 128:p0 // 128 + NBK, :], "VB", w=["VB"])
                        halves(lambda hf, h=h, p0=p0: ew_sweepA(h, p0, hf))
                        order = list(reversed(range(NBK)))
                        khs = {order[0]: khT_block(order[0])}
                        for idx, b in enumerate(order):
                            if idx + 1 < NBK:
                                khs[order[idx + 1]] = khT_block(order[idx + 1])
                            kh, kht = khs.pop(b)
                            for part in (1, 0):
                                cl = 2 * b + part
                                cg = p0 // 64 + cl
                                scur, scurt = state_step(kh, kht, b, part, cl, scur, scurt)
                                if 1 <= cg <= L // 64:
                                    T.op("act", nc.scalar.copy, KW(out=SPV[:, cg - 1, :], in_=scur[:]), r=[scurt], w=[("SPV", cg - 1)])
                    scur, scurt = SF.next()
                    T.op("pool", nc.gpsimd.memset, KW(scur[:], 0.0), w=[scurt])
                    for pi in range(L // PW):
                        p0 = pi * PW
                        T.dma("sp", VB[:], s_vb[h, :, p0 // 128:p0 // 128 + NBK, :], "VB", w=["VB"])
                        halves(lambda hf, h=h, p0=p0: ew_sweepB(h, p0, hf))
                        for b in range(NBK):
                            bs = slice(b * 128, (b + 1) * 128)
                            for dirn, (kt_, qt_, msk, kn, qn) in enumerate(((KTt, QTt, MSKF, "KT", "QT"), (KT2, QT2, MSKB, "KT2", "QT2"))):
                                at = ("aT", dirn)
                                T.op("pe", nc.tensor.matmul, KW(PS[2 + dirn][:, 0:128], lhsT=kt_[:, bs], rhs=qt_[:, bs], start=True, stop=True), r=[(kn, hb_of(b)), (qn, hb_of(b))], w=[at])
                                T.op("dve", nc.vector.tensor_tensor, KW(out=AMA[:, b, dirn, :], in0=PS[2 + dirn][:, 0:128], in1=msk, op=ALU.mult), r=[at], w=[("AMA", b)])
                        khs = {0: khT_block(0)}
                        for b in range(NBK):
                            if b + 1 < NBK:
                                khs[b + 1] = khT_block(b + 1)
                            kh, kht = khs.pop(b)
                            for part in (0, 1):
                                cl = 2 * b + part
                                T.op("act", nc.scalar.copy, KW(out=SFA[:, cl, :], in_=scur[:]), r=[scurt], w=[("SFA", cl)])
                                scur, scurt = state_step(kh, kht, b, part, cl, scur, scurt)
                        pendg = None

                        def finish_group(g):
                            t0_, ls = g
                            T.op("pe", nc.tensor.matmul, KW(PS[5][:, :], lhsT=ONES, rhs=SQ[:], start=True, stop=True), r=["sq"], w=["psM"])
                            T.op("act", nc.scalar.activation, KW(out=RS[:], in_=PS[5][:, :], func=AF.Ln, bias=EPSC, scale=1.0 / 128), r=["psM"], w=["RS"])
                            T.op("act", nc.scalar.activation, KW(out=RS[:], in_=RS[:], func=AF.Exp, scale=-0.5), r=["RS"], w=["RS"])
                            T.op("dve", nc.vector.scalar_tensor_tensor, KW(out=Y1[:], in0=OS[:], scalar=CF[:, C_GHN:C_GHN + 1], in1=RS[:], op0=ALU.mult, op1=ALU.mult), r=["OS", "RS"], w=["Y1"])
                            ys, yst = YS.next()
                            T.op("dve", nc.vector.tensor_tensor, KW(out=ys[:], in0=Y1[:], in1=SG[:, ls], op=ALU.mult), r=["Y1", ("SG", ls.start // HW_)], w=[yst])
                            T.dma("pool", s_yb[h, :, t0_:t0_ + 512], ys[:], "yt%d" % yst[1], r=[yst], w=[("yb", h)])
                        for b in range(NBK):
                            col = (b % 4) * 128
                            obank = PS[4]
                            obt = ("psOh", 0)
                            T.op("pe", nc.tensor.matmul, KW(obank[:, col:col + 128], lhsT=VB[:, b, :], rhs=AMA[:, b, 0, :], start=True, stop=False), r=["VB", ("AMA", b)], w=[obt])
                            T.op("pe", nc.tensor.matmul, KW(obank[:, col:col + 128], lhsT=VB[:, b, :], rhs=AMA[:, b, 1, :], start=False, stop=False), r=["VB", ("AMA", b)], w=[obt])
                            for part in (0, 1):
                                cl = 2 * b + part
                                cg = p0 // 64 + cl
                                cs = slice(cl * 64, cl * 64 + 64)
                                oc = slice(col + part * 64, col + part * 64 + 64)
                                T.op("pe", nc.tensor.matmul, KW(obank[:, oc], lhsT=SFA[:, cl, :], rhs=QE[:, cs], start=False, stop=False), r=[("SFA", cl), ("QE", hb_of(b))], w=[obt])
                                T.op("pe", nc.tensor.matmul, KW(obank[:, oc], lhsT=SPV[:, cg, :], rhs=QE2[:, cs], start=False, stop=(part == 1)), r=[("SPV", cg), ("QE2", hb_of(b))], w=[obt])
                            if b % 4 == 3:
                                if pendg is not None:
                                    finish_group(pendg)
                                T.op("dve", nc.vector.tensor_copy, KW(out=OS[:], in_=obank[:, :]), r=[obt], w=["OS"])
                                T.op("act", nc.scalar.activation, KW(out=SQ[:], in_=OS[:], func=AF.Square), r=["OS"], w=["sq"])
                                pendg = (p0 + (b - 3) * 128, slice((b - 3) * 128, (b + 1) * 128))
                        finish_group(pendg)
                T.barrier()

            with ExitStack() as ph:
                X = sb(ph, "p4X", [128, 4, D], F32)
                U = sb(ph, "p4U", [128, 4, D], F32)
                HTb = sb(ph, "p4HT", [128, KC, 512], BF16)
                SCR = sb(ph, "p4SCR", [128, 24, 512], BF16)
                GAB = Rot("GAB", [sb(ph, "p4gab%d" % i, [128, 2, 512], BF16) for i in range(8)])
                QX = sb(ph, "p4QX", [128, 4, 512], BF16)
                OX = sb(ph, "p4OX", [128, 4, 512], BF16)
                KXT = sb(ph, "p4KXT", [128, 4, NMEM], BF16)
                VX = sb(ph, "p4VX", [128, 2, 512], BF16)
                WP = Rot("WP", [sb(ph, "p4wp%d" % i, [128, PG, 512], BF16) for i in range(3)])
                GP = Rot("GP", [sb(ph, "p4gp", [128, D], F32)])
                SSP = sb(ph, "p4ssp", [128, 16], F32)
                JKo = None if OPT_PN else sb(ph, "p4jko", [128, D], F32)
                JQ = Rot("JQ", [sb(ph, "p4jq%d" % i, [128, 512], BF16) for i in range(2)])
                SS = Rot("SS", [sb(ph, "p4ss%d" % i, [128, 4], F32) for i in range(2)])
                XN = Rot("XN", [sb(ph, "p4xn", [128, D], BF16)])
                PP = Rot("PP", [sb(ph, "p4pp%d" % i, [128, 512], BF16) for i in range(2)])
                RZ = sb(ph, "p4rz", [128, 512], F32)
                SGt = Rot("SGt", [sb(ph, "p4sg%d" % i, [128, 512], F32) for i in range(4)])
                PSR = Rot("PS", PS)

                def gemm_tok(name, act, acttok, nkc, ncp, evac, kc_off=0, kg_list=None):
                    nkg = -(-nkc // PG)
                    for cp in range(ncp):
                        banks = [PSR.next() for _ in range(4)]
                        for kg in range(nkg):
                            g_n = min(PG, nkc - kg * PG)
                            wb, wt = load_panel(WP, name, cp, kg + kc_off // PG, g_n)
                            for sub in range(4):
                                pb, pbt = banks[sub]
                                for g in range(g_n):
                                    kc = kg * PG + g
                                    st, sp_ = (kc == 0), (kc == nkc - 1)
                                    T.op("pe", nc.tensor.matmul, KW(pb[:, :], lhsT=act[:, kc, sub * 128:(sub + 1) * 128], rhs=wb[:, g, :], start=st, stop=sp_), r=[acttok(kc), wt], w=[pbt])
                        for sub in range(4):
                            evac(sub, cp, banks[sub][0], banks[sub][1])

                def gemm_feat(name, act, acttok, nkc, cp, evac, ntok=512):
                    nkg = -(-nkc // PG)
                    banks = [PSR.next() for _ in range(4)]
                    for kg in range(nkg):
                        g_n = min(PG, nkc - kg * PG)
                        wb, wt = load_panel(WP, name, cp, kg, g_n)
                        for j in range(4):
                            pb, pbt = banks[j]
                            for g in range(g_n):
                                kc = kg * PG + g
                                st, sp_ = (kc == 0), (kc == nkc - 1)
                                T.op("pe", nc.tensor.matmul, KW(pb[:, 0:ntok], lhsT=wb[:, g, j * 128:(j + 1) * 128], rhs=act[:, kc, 0:ntok], start=st, stop=sp_), r=[acttok(kc), wt], w=[pbt])
                    for j in range(4):
                        evac(j, banks[j][0], banks[j][1])

                def post_norm_residual(grow):
                    gp, gpt = GP.next()
                    T.dma("sp", gp[:], bass.AP(tensor=rows.tensor, offset=grow * D, ap=[[0, 128], [1, D]]), "gp", w=[gpt])
                    for sub in range(4):
                        ss, sst = SS.next()
                        if not OPT_PN:
                            T.op("act", nc.scalar.activation, KW(out=JKo[:], in_=U[:, sub, :], func=AF.Square), r=[("U", sub)], w=["JKo"])
                            T.op("dve", nc.vector.tensor_reduce, KW(out=ss[:, 0:1], in_=JKo[:], axis=AX.X, op=ALU.add), r=["JKo"], w=[sst])
                            T.op("act", nc.scalar.activation, KW(out=ss[:, 1:2], in_=ss[:, 0:1], func=AF.Ln, bias=EPSC, scale=1.0 / D), r=[sst], w=[sst])
                            T.op("act", nc.scalar.activation, KW(out=ss[:, 2:3], in_=ss[:, 1:2], func=AF.Exp, scale=-0.5), r=[sst], w=[sst])
                            T.op("dve", nc.vector.scalar_tensor_tensor, KW(out=JKo[:], in0=U[:, sub, :], scalar=ss[:, 2:3], in1=gp[:], op0=ALU.mult, op1=ALU.mult), r=[("U", sub), sst, gpt], w=["JKo"])
                            T.op("pool", nc.gpsimd.tensor_tensor, KW(out=X[:, sub, :], in0=X[:, sub, :], in1=JKo[:], op=ALU.add), r=["JKo", ("X", sub)], w=[("X", sub)])
                            continue
                        T.op("dve", nc.vector.tensor_reduce, KW(out=ss[:, 0:1], in_=SSP[:, sub * 4:(sub + 1) * 4], axis=AX.X, op=ALU.add), r=[("SSP", sub)], w=[sst])
                        T.op("act", nc.scalar.activation, KW(out=ss[:, 1:2], in_=ss[:, 0:1], func=AF.Ln, bias=EPSC, scale=1.0 / D), r=[sst], w=[sst])
                        T.op("act", nc.scalar.activation, KW(out=ss[:, 2:3], in_=ss[:, 1:2], func=AF.Exp, scale=-0.5), r=[sst], w=[sst])
                        T.op("dve", nc.vector.scalar_tensor_tensor, KW(out=U[:, sub, :], in0=U[:, sub, :], scalar=ss[:, 2:3], in1=gp[:], op0=ALU.mult, op1=ALU.mult), r=[("U", sub), sst, gpt], w=[("U", sub)])
                        T.op("dve", nc.vector.tensor_tensor, KW(out=X[:, sub, :], in0=X[:, sub, :], in1=U[:, sub, :], op=ALU.add), r=[("U", sub), ("X", sub)], w=[("X", sub)])

                def sq_piece(sub, cp, src, srctok):
                    if not OPT_PN:
                        return
                    jq, jqt = JQ.next()
                    if OPT_ACC:
                        T.op("act", nc.scalar.activation, KW(out=jq[:], in_=src, func=AF.Square, accum_out=SSP[:, sub * 4 + cp:sub * 4 + cp + 1]), r=[srctok], w=[jqt, ("SSP", sub)])
                    else:
                        T.op("act", nc.scalar.activation, KW(out=jq[:], in_=src, func=AF.Square), r=[srctok], w=[jqt])
                        T.op("dve", nc.vector.tensor_reduce, KW(out=SSP[:, sub * 4 + cp:sub * 4 + cp + 1], in_=jq[:], axis=AX.X, op=ALU.add), r=[jqt], w=[("SSP", sub)])

                def pre_norm(gcol):
                    for sub in range(4):
                        norm_T((SS, XN), X[:, sub, :], ("X", sub), gcol, HTb, "HT", sub * 128)

                for mb in range(2):
                    xrow = ji * NMEM + mb * 128
                    T.dma("sp", U[:, mb, :], mems[xrow:xrow + 128, :], "memld%d" % mb, w=[("U", mb)])
                    norm_T((SS, XN), U[:, mb, :], ("U", mb), C_GMEM, HTb, "HT", mb * 128)

                def ev_kx(j, pb, pbt):
                    T.op("dve", nc.vector.tensor_copy, KW(out=KXT[:, j, :], in_=pb[:, 0:NMEM]), r=[pbt], w=["KXT"])
                gemm_feat("w_kv_x", HTb, lambda kc: "HT", KC, 0, ev_kx, ntok=NMEM)
                banks = [PSR.next() for _ in range(2)]
                for kg in range(2):
                    wb, wt = load_panel(WP, "w_kv_x", 1, kg)
                    for mb in range(2):
                        pb, pbt = banks[mb]
                        for g in range(PG):
                            kc = kg * PG + g
                            T.op("pe", nc.tensor.matmul, KW(pb[:, :], lhsT=HTb[:, kc, mb * 128:(mb + 1) * 128], rhs=wb[:, g, :], start=(kc == 0), stop=(kc == KC - 1)), r=["HT", wt], w=[pbt])
                for mb in range(2):
                    pb, pbt = banks[mb]
                    T.op("act", nc.scalar.copy, KW(out=VX[:, mb, :], in_=pb[:, :]), r=[pbt], w=["VX"])
                for t in range(L // 512):
                    t0 = t * 512
                    for sub in range(4):
                        T.dma("pool", X[:, sub, :], xs[lo + t0 + sub * 128: lo + t0 + (sub + 1) * 128, :], "xld%d" % sub, w=[("X", sub)])
                    T.dma("sp", SCR[:, 0:8, :], s_ya[:, :, t0:t0 + 512].rearrange("h p t -> p h t"), "scrA", w=[("SCR", i) for i in range(8)])
                    T.dma("sp", SCR[:, 8:16, :], s_yb[:, :, t0:t0 + 512].rearrange("h p t -> p h t"), "scrB", w=[("SCR", i) for i in range(8, 16)])
                    for cp in range(4):
                        resA = {}
                        gabs = {}
                        for j in range(4):
                            gab, gabt = GAB.next()
                            T.dma("pool", gab[:, 0, :], s_gt[cp * 4 + j, :, t0:t0 + 512], "gab%da" % gabt[1], w=[(gabt, 0)])
                            T.dma("pool", gab[:, 1, :], s_gt[16 + cp * 4 + j, :, t0:t0 + 512], "gab%db" % gabt[1], w=[(gabt, 1)])
                            gabs[j] = (gab, gabt)

                        def ev_a(j, pb, pbt, cp=cp, resA=resA, gabs=gabs):
                            fc = cp * 4 + j
                            gab, gabt = gabs[j]
                            mt, mtt = U[:, j, cp * 512:(cp + 1) * 512], ("U", j)
                            T.op("dve", nc.vector.tensor_tensor, KW(out=mt, in0=pb[:, :], in1=gab[:, 0, :], op=ALU.mult), r=[pbt, (gabt, 0)], w=[mtt])
                            resA[j] = (mt, mtt, gab, gabt)

                        def ev_b(j, pb, pbt, cp=cp, resA=resA):
                            fc = cp * 4 + j
                            mt, mtt, gab, gabt = resA[j]
                            mt2, mtt2 = SGt.next()
                            T.op("dve", nc.vector.tensor_tensor, KW(out=mt2[:], in0=pb[:, :], in1=gab[:, 1, :], op=ALU.mult), r=[pbt, (gabt, 1)], w=[mtt2])
                            T.op("dve", nc.vector.tensor_tensor, KW(out=HTb[:, fc, :], in0=mt, in1=mt2[:], op=ALU.add), r=[mtt, mtt2], w=["HT"])
                        gemm_feat("w_ba", SCR, lambda kc: ("SCR", kc), 8, cp, ev_a)
                        gemm_feat("w_bb", SCR[:, 8:16, :], lambda kc: ("SCR", 8 + kc), 8, cp, ev_b)

                    def ev_copy(sub, cp, pb, pbt):
                        T.op("dve", nc.vector.tensor_copy, KW(out=U[:, sub, cp * 512:(cp + 1) * 512], in_=pb[:, :]), r=[pbt], w=[("U", sub)])

                    def ev_u(sub, cp, pb, pbt):
                        ev_copy(sub, cp, pb, pbt)
                        sq_piece(sub, cp, U[:, sub, cp * 512:(cp + 1) * 512], ("U", sub))
                    gemm_tok("w_out", HTb, lambda kc: "HT", KC, 4, ev_u)
                    post_norm_residual(0)
                    pre_norm(C_GPX)

                    def ev_q(j, pb, pbt):
                        T.op("act", nc.scalar.activation, KW(out=QX[:, j, :], in_=pb[:, :], func=AF.Copy, scale=128 ** -0.5), r=[pbt], w=["QX"])
                    gemm_feat("w_q_x", HTb, lambda kc: "HT", KC, 0, ev_q)
                    for hx in range(4):
                        ob, obt = PSR.next()
                        zb_, zbt = PSR.next()
                        for mb in range(2):
                            sbk, sbt_ = PSR.next()
                            pp, ppt = PP.next()
                            T.op("pe", nc.tensor.matmul, KW(sbk[:, :], lhsT=KXT[:, hx, mb * 128:(mb + 1) * 128], rhs=QX[:, hx, :], start=True, stop=True), r=["KXT", "QX"], w=[sbt_])
                            T.op("act", nc.scalar.activation, KW(out=pp[:], in_=sbk[:, :], func=AF.Exp), r=[sbt_], w=[ppt])
                            T.op("pe", nc.tensor.matmul, KW(ob[:, :], lhsT=VX[:, mb, hx * 128:(hx + 1) * 128], rhs=pp[:], start=(mb == 0), stop=(mb == 1)), r=["VX", ppt], w=[obt])
                            T.op("pe", nc.tensor.matmul, KW(zb_[:, :], lhsT=ONES, rhs=pp[:], start=(mb == 0), stop=(mb == 1)), r=[ppt], w=[zbt])
                        T.op("dve", nc.vector.reciprocal, KW(out=RZ[:], in_=zb_[:, :]), r=[zbt], w=["RZ"])
                        T.op("dve", nc.vector.tensor_tensor, KW(out=OX[:, hx, :], in0=ob[:, :], in1=RZ[:], op=ALU.mult), r=[obt, "RZ"], w=["OX"])
                    gemm_tok("w_o_x", OX, lambda kc: "OX", 4, 4, ev_u)
                    post_norm_residual(1)
                    pre_norm(C_GPF)
                    for half, (c_lo, c_hi) in enumerate(((0, 24), (24, 44))):
                        for p in range(c_lo // 4, c_hi // 4):
                            resG = {}

                            def ev_g(j, pb, pbt, resG=resG):
                                sg, sgt = SGt.next()
                                T.op("act", nc.scalar.activation, KW(out=sg[:], in_=pb[:, :], func=AF.Silu), r=[pbt], w=[sgt])
                                resG[j] = (sg, sgt)

                            def ev_up(j, pb, pbt, p=p, resG=resG, c_lo=c_lo):
                                sg, sgt = resG[j]
                                slot = p * 4 + j - c_lo
                                T.op("dve", nc.vector.tensor_tensor, KW(out=SCR[:, slot, :], in0=pb[:, :], in1=sg[:], op=ALU.mult), r=[pbt, sgt], w=[("SCR", slot)])
                            gemm_feat("w_gu", HTb, lambda kc: "HT", KC, p, ev_g)
                            gemm_feat("w_gu", HTb, lambda kc: "HT", KC, 11 + p, ev_up)

                        def ev_d(sub, cp, pb, pbt, half=half):
                            if half == 0:
                                ev_copy(sub, cp, pb, pbt)
                            else:
                                T.op("dve", nc.vector.tensor_tensor, KW(out=U[:, sub, cp * 512:(cp + 1) * 512], in0=pb[:, :], in1=U[:, sub, cp * 512:(cp + 1) * 512], op=ALU.add), r=[pbt, ("U", sub)], w=[("U", sub)])
                                sq_piece(sub, cp, U[:, sub, cp * 512:(cp + 1) * 512], ("U", sub))
                        gemm_tok("w_down", SCR, lambda kc: ("SCR", kc), c_hi - c_lo, 4, ev_d, kc_off=c_lo)
                    post_norm_residual(2)
                    for sub in range(4):
                        T.dma("pool", y[lo + t0 + sub * 128: lo + t0 + (sub + 1) * 128, :], X[:, sub, :], "yst%d" % sub, r=[("X", sub)], w=[("y", t, sub)])
                T.barrier()
            lo += L
            ro += R
        T.emit()
    return nc


def _bucket_np(rel):
    try:
        import jax
        import jax.numpy as jnp
        with jax.default_device(jax.devices("cpu")[0]):
            r = jnp.asarray(rel, dtype=jnp.int32)
            nb, me = 16, 8
            n = jnp.abs(r)
            nf = jnp.maximum(n, 1).astype(jnp.float32)
            large = me + (jnp.log(nf / me) / math.log(128 / me) * (nb - me)).astype(jnp.int32)
            large = jnp.minimum(large, nb - 1)
            return np.asarray(jnp.where(r > 0, nb, 0) + jnp.where(n < me, n, large))
    except Exception:
        rel = np.asarray(rel, np.int64)
        n = np.abs(rel)
        nf = np.maximum(n, 1).astype(np.float32)
        large = 8 + (np.log(nf / np.float32(8)) / np.float32(math.log(16)) * np.float32(8)).astype(np.int32)
        large = np.minimum(large, 15)
        return np.where(rel > 0, 16, 0) + np.where(n < 8, n, large)


def make_consts():
    bk = _bucket_np(640 - np.arange(1280))
    e1h = np.zeros((32, 1280), np.float32)
    e1h[bk, np.arange(1280)] = 1.0
    cmask = np.ones((128, 2048), np.float32)
    cmask[:, ::64] = 0.0
    i = np.arange(128)
    same = (i[:, None] // 64) == (i[None, :] // 64)
    mf = (same & (i[:, None] <= i[None, :])).astype(np.float32)
    mb = (same & (i[:, None] >= i[None, :])).astype(np.float32)
    cb = np.concatenate([np.eye(128, dtype=np.float32), np.ones((128, 128), np.float32), mf, mb], axis=1).astype(ml_dtypes.bfloat16)
    return e1h, cmask, cb


def core_inputs(flip, x_loc_list, x_rem_list, mem_list, P):
    fm = lambda v, n: np.ascontiguousarray(np.asarray(v, np.float32).reshape(n, 128).T)
    w_in = np.asarray(P["w_in"][0], np.float32)
    lbl = np.asarray(P["hgrn_lb_logits"], np.float32)
    rb = np.asarray(P["rel_bias"], np.float32)
    if flip:
        w_in = w_in.copy()
        w_in[:, 4096:5120], w_in[:, 5120:6144] = P["w_in"][0][:, 5120:6144], P["w_in"][0][:, 4096:5120]
        lbl = lbl[::-1]
        rb2 = rb.copy()
        rb2[1:16], rb2[17:32] = rb[17:32], rb[1:16]
        rb = rb2
        x_loc_list = [a[::-1] for a in x_loc_list]
        x_rem_list = [a[::-1] for a in x_rem_list]
    cst = np.zeros((128, C_NCOL), np.float32)
    cst[:, C_GPM:C_GPM + 16] = fm(P["g_pre_mix"][0], 16)
    cst[:, C_GPX:C_GPX + 16] = fm(P["g_pre_x"][0], 16)
    cst[:, C_GPF:C_GPF + 16] = fm(P["g_pre_ffn"][0], 16)
    cst[:, C_GMEM:C_GMEM + 16] = fm(P["g_mem"][0], 16)
    cst[:, C_BM:C_BM + 32] = fm(P["b_merge"][0], 32)
    cst[:, C_LBL:C_LBL + 32] = np.ascontiguousarray(lbl.reshape(2, 2, 8, 128).transpose(3, 0, 1, 2).reshape(128, 32))
    cst[:, C_GSUB] = np.asarray(P["g_subln"][0], np.float32)
    cst[:, C_GHN] = np.asarray(P["g_hgrn_norm"][0], np.float32)
    cst[:, C_EPS] = EPS
    e1h, cmask, cb = make_consts()
    xs = np.ascontiguousarray(np.concatenate([np.asarray(a, np.float32) for a in (list(x_loc_list) + list(x_rem_list)) if a.shape[0] > 0], axis=0))
    m = {
        "xs": xs,
        "mems": np.ascontiguousarray(np.concatenate([np.asarray(a, np.float32) for a in mem_list], axis=0)),
        "w_in": np.ascontiguousarray(w_in),
        "w_ba": np.ascontiguousarray(P["w_branch_a"][0], dtype=np.float32),
        "w_bb": np.ascontiguousarray(P["w_branch_b"][0], dtype=np.float32),
        "w_out": np.ascontiguousarray(P["w_out"][0], dtype=np.float32),
        "w_q_x": np.ascontiguousarray(P["w_q_x"][0], dtype=np.float32),
        "w_kv_x": np.ascontiguousarray(P["w_kv_x"][0], dtype=np.float32),
        "w_o_x": np.ascontiguousarray(P["w_o_x"][0], dtype=np.float32),
        "w_gu": np.ascontiguousarray(P["w_gate_up"][0], dtype=np.float32),
        "w_down": np.ascontiguousarray(P["w_down"][0], dtype=np.float32),
        "cst_f32": cst,
        "rows_f32": np.ascontiguousarray(np.stack([P["g_post_mix"][0], P["g_post_x"][0], P["g_post_ffn"][0]]).astype(np.float32)),
        "lamv": np.ascontiguousarray(np.concatenate([P["lam_q1"][0], P["lam_k1"][0], P["lam_q2"][0], P["lam_k2"][0]]).astype(np.float32).reshape(1, 256)),
        "rel_bias": np.ascontiguousarray(rb),
        "e1h": e1h,
        "cmask": cmask,
        "cst_bf": cb,
    }
    return m


def kernel(**inp):
    P = inp
    xp = np.asarray(inp["x_prompt"], np.float32)
    xsm = np.asarray(inp["x_sample"], np.float32)
    mp = np.asarray(inp["mem_prompt"], np.float32)
    ms = np.asarray(inp["mem_sample"], np.float32)
    NCORE = 8
    SP = xp.shape[1]
    SS_ = xsm.shape[1]
    HS = SS_ // 2
    jobs = [(SP, 0), (SP, 0), (HS, HS)]
    nc = build(jobs)
    in_maps = []
    for c in range(NCORE):
        flip = (c % 2 == 1)
        s = c // 2
        xq = xsm[s]
        if not flip:
            sl, sr = xq[:HS], xq[HS:]
        else:
            sl, sr = xq[HS:], xq[:HS]
        empty = np.zeros((0, D), np.float32)
        in_maps.append(core_inputs(flip, [xp[2 * c], xp[2 * c + 1], sl], [empty, empty, sr], [mp[2 * c], mp[2 * c + 1], ms[s]], P))
    res = run_bass_kernel_spmd(nc, in_maps, core_ids=list(range(NCORE)))
    yp = np.zeros_like(xp)
    ysm = np.zeros_like(xsm)
    for c in range(NCORE):
        yc = np.asarray(res.results[c]["y"], np.float32)
        flip = (c % 2 == 1)
        s = c // 2
        a, b, cc = yc[:SP], yc[SP:2 * SP], yc[2 * SP:]
        if flip:
            a, b, cc = a[::-1], b[::-1], cc[::-1]
            ysm[s, HS:] = cc
        else:
            ysm[s, :HS] = cc
        yp[2 * c] = a
        yp[2 * c + 1] = b
    return (yp, ysm)
```

```python
import math
from contextlib import ExitStack
import numpy as np
import ml_dtypes
import concourse.bass as bass
import concourse.mybir as mybir
from concourse.bass_utils import run_bass_kernel_spmd

F32 = mybir.dt.float32
BF16 = mybir.dt.bfloat16
AF = mybir.ActivationFunctionType
ALU = mybir.AluOpType
AX = mybir.AxisListType

D = 2048
KC = 16
NIN = 12288
DFF = 5632
NMEM = 256
EPS = 1e-6
LAM_INIT = 0.8 - 0.6 * math.exp(0.0)
PG = 8
OPT_ACC = True
OPT_PTF = True
OPT_PN = True
SAME_ENGINE_SYNC = True

WSPECS = [("w_in", 2048, 12288), ("w_ba", 1024, 2048), ("w_bb", 1024, 2048), ("w_out", 2048, 2048),
          ("w_q_x", 2048, 512), ("w_kv_x", 2048, 1024), ("w_o_x", 512, 2048),
          ("w_gu", 2048, 11264), ("w_down", 5632, 2048)]

C_GPM, C_GPX, C_GPF, C_GMEM, C_BM, C_LBL, C_GSUB, C_GHN, C_EPS, C_ZERO, C_NCOL = 0, 16, 32, 48, 64, 96, 128, 129, 130, 131, 132


class Tr:
    def __init__(self, nc, es):
        self.nc = nc
        self.es = es
        self.E = {"pe": nc.tensor, "act": nc.scalar, "dve": nc.vector, "pool": nc.gpsimd, "sp": nc.sync}
        self.ops = []

    def op(self, eng, fn, ak, r=(), w=()):
        self.ops.append(dict(k="c", eng=eng, fn=fn, ak=ak, r=tuple(r), w=tuple(w), need=False))

    def dma(self, q, out, in_, key, r=(), w=()):
        self.ops.append(dict(k="d", eng=q, out=out, in_=in_, key=key, r=tuple(r), w=tuple(w), need=True))

    def barrier(self):
        self.ops.append(dict(k="b"))

    def emit(self):
        ops = self.ops
        last_w, readers = {}, {}
        last_on_eng = {}
        for i, o in enumerate(ops):
            if o["k"] == "b":
                o["lasts"] = dict(last_on_eng)
                for j in last_on_eng.values():
                    ops[j]["need"] = True
                last_w.clear()
                readers.clear()
                continue
            deps = set()
            for t in o["r"]:
                if t in last_w:
                    deps.add(last_w[t])
            for t in o["w"]:
                if t in last_w:
                    deps.add(last_w[t])
                deps.update(readers.get(t, ()))
            deps.discard(i)
            o["deps"] = deps
            for t in o["r"]:
                readers.setdefault(t, []).append(i)
            for t in o["w"]:
                last_w[t] = i
                readers[t] = []
            for d in deps:
                ops[d]["need"] = True
            if o["k"] == "c":
                last_on_eng[o["eng"]] = i
        esem = {e: self.es.enter_context(self.nc.semaphore("se_" + e)) for e in ("pe", "act", "dve", "pool")}
        ecnt = {e: 0 for e in esem}
        ksem, kcnt = {}, {}
        waited = {e: {} for e in self.E}

        def wait(e, sem, name, val):
            if waited[e].get(name, 0) < val:
                self.E[e].wait_ge(sem, val)
                waited[e][name] = val

        for i, o in enumerate(ops):
            if o["k"] == "b":
                for e in self.E:
                    for f, j in o["lasts"].items():
                        if f != e:
                            wait(e, esem[f], "e" + f, ops[j]["sig"][2])
                    for kk, c in kcnt.items():
                        wait(e, ksem[kk], "k" + kk, c)
                continue
            e = o["eng"]
            for d in sorted(o["deps"]):
                p = ops[d]
                if p["k"] == "c" and o["k"] == "c" and p["eng"] == e and (e == "pe" or not SAME_ENGINE_SYNC):
                    continue
                sem, name, val = p["sig"]
                wait(e, sem, name, val)
            if o["k"] == "c":
                ins = o["fn"](*o["ak"][0], **o["ak"][1])
                if o["need"]:
                    ecnt[e] += 1
                    ins.then_inc(esem[e], 1)
                    o["sig"] = (esem[e], "e" + e, ecnt[e])
            else:
                key = o["key"]
                if key not in ksem:
                    ksem[key] = self.es.enter_context(self.nc.semaphore("sk_" + key))
                    kcnt[key] = 0
                kcnt[key] += 16
                self.E[e].dma_start(out=o["out"], in_=o["in_"]).then_inc(ksem[key], 16)
                o["sig"] = (ksem[key], "k" + key, kcnt[key])
        for f in esem:
            if ecnt[f]:
                wait("pool", esem[f], "e" + f, ecnt[f])
        for kk, c in kcnt.items():
            wait("pool", ksem[kk], "k" + kk, c)
        self.nsem = len(ksem) + 4


def KW(*a, **k):
    return (a, k)


class Rot:
    def __init__(self, name, bufs):
        self.name, self.bufs, self.i = name, bufs, 0

    def next(self):
        j = self.i % len(self.bufs)
        self.i += 1
        return self.bufs[j], (self.name, j)


def build(jobs):
    nc = bass.Bass("TRN2", target_bir_lowering=False)
    NLOC = sum(L for L, R in jobs)
    NREM = sum(R for L, R in jobs)
    NJ = len(jobs)
    TMAX = max(L + R for L, R in jobs)
    LMAX = max(L for L, R in jobs)

    def din(name, shape, dt=F32):
        return nc.dram_tensor(name, list(shape), dt, kind="ExternalInput").ap()

    def dscr(name, shape, dt):
        return nc.dram_tensor(name, list(shape), dt, kind="Internal").ap()

    xs = din("xs", [NLOC + NREM, D])
    mems = din("mems", [NJ * NMEM, D])
    wsrc = {n: din(n, [K, N]) for n, K, N in WSPECS}
    cstf = din("cst_f32", [128, C_NCOL])
    rows = din("rows_f32", [3, D])
    lamv = din("lamv", [1, 256])
    relb = din("rel_bias", [32, 8])
    e1h = din("e1h", [32, 1280])
    cmaskd = din("cmask", [128, 2048])
    cstb = din("cst_bf", [128, 512], BF16)
    y = nc.dram_tensor("y", [NLOC, D], F32, kind="ExternalOutput").ap()

    wpan = {}
    for n, K, N in WSPECS:
        nkg = -(-(K // 128) // PG)
        wpan[n] = dscr("wb_" + n, [N // 512, nkg, 128, PG, 512], BF16)
    s_qa = dscr("s_qa", [8, 128, LMAX], BF16)
    s_ka = dscr("s_ka", [8, 128, TMAX], BF16)
    s_va = dscr("s_va", [8, 128, TMAX // 128, 128], BF16)
    s_qb = dscr("s_qb", [8, 128, LMAX], BF16)
    s_zf = dscr("s_zf", [8, 128, LMAX], F32)
    s_zb = dscr("s_zb", [8, 128, TMAX], F32)
    s_vb = dscr("s_vb", [8, 128, TMAX // 128, 128], BF16)
    s_sg = dscr("s_sg", [8, 128, LMAX], BF16)
    s_gt = dscr("s_gt", [32, 128, LMAX], BF16)
    s_ya = dscr("s_ya", [8, 128, LMAX], BF16)
    s_yb = dscr("s_yb", [8, 128, LMAX], BF16)
    s_tb0 = dscr("s_tb0", [8, 1280], F32)
    s_rep = dscr("s_rep", [8, 129, 1280], F32)

    es = ExitStack()
    with es:
        T = Tr(nc, es)

        _uid = [0]

        def sb(st, name, shape, dt):
            _uid[0] += 1
            return st.enter_context(nc.sbuf_tensor("%s_%d" % (name, _uid[0]), list(shape), dt))

        CF = sb(es, "CF", [128, C_NCOL], F32)
        CB = sb(es, "CB", [128, 512], BF16)
        LBV = sb(es, "LBV", [128, 32], F32)
        LAMT = sb(es, "LAMT", [128, 8], F32)
        FARB = sb(es, "FARB", [128, 16], F32)
        ONE1 = sb(es, "ONE1", [128, 1], F32)
        IDB = CB[:, 0:128]
        ONES = CB[:, 128:256]
        MSKF = CB[:, 256:384]
        MSKB = CB[:, 384:512]
        EPSC = CF[:, C_EPS:C_EPS + 1]
        PT = [es.enter_context(nc.psum_tensor("PT%d" % i, [128, 1024], BF16)) for i in range(2)]
        PS = [es.enter_context(nc.psum_tensor("PS%d" % i, [128, 512], F32)) for i in range(6)]

        T.dma("sp", CF[:], cstf[:, :], "CF", w=["CF"])
        T.dma("sp", CB[:], cstb[:, :], "CB", w=["CB"])
        T.op("dve", nc.vector.memset, KW(ONE1[:], 1.0), w=["ONE1"])
        def convert(names, key):
            for n, K, N in WSPECS:
                if n not in names:
                    continue
                nkc = K // 128
                for cp in range(N // 512):
                    for kg in range(-(-nkc // PG)):
                        g = min(PG, nkc - kg * PG)
                        src = bass.AP(tensor=wsrc[n].tensor, offset=kg * PG * 128 * N + cp * 512,
                                      ap=[[N, 128], [128 * N, g], [1, 512]])
                        T.dma("pool", wpan[n][cp, kg, :, 0:g, :], src, key, w=[("wb", n, cp, kg)])
        convert(("w_in",), "cv")
        with ExitStack() as ph:
            LQ = sb(ph, "LQ", [128, 256], F32)
            LP = sb(ph, "LP", [128, 128], F32)
            LS = sb(ph, "LS", [128, 4], F32)
            RB = sb(ph, "RB", [32, 8], F32)
            EH = sb(ph, "EH", [32, 1280], F32)
            TB = sb(ph, "TB", [8, 1280], F32)
            T.dma("sp", LQ[:], lamv.partition_broadcast(128) if False else bass.AP(tensor=lamv.tensor, offset=0, ap=[[0, 128], [1, 256]]), "LQ", w=["LQ"])
            T.dma("sp", RB[:], relb[:, :], "RB", w=["RB"])
            T.dma("sp", EH[:], e1h[:, :], "EH", w=["EH"])
            T.dma("sp", FARB[:, 0:8], bass.AP(tensor=relb.tensor, offset=15 * 8, ap=[[0, 128], [1, 8]]), "FARB", w=["FARB"])
            T.dma("sp", FARB[:, 8:16], bass.AP(tensor=relb.tensor, offset=31 * 8, ap=[[0, 128], [1, 8]]), "FARB", w=["FARB"])
            T.op("dve", nc.vector.tensor_tensor, KW(out=LP[:, 0:64], in0=LQ[:, 0:64], in1=LQ[:, 64:128], op=ALU.mult), r=["LQ"], w=["LP"])
            T.op("dve", nc.vector.tensor_tensor, KW(out=LP[:, 64:128], in0=LQ[:, 128:192], in1=LQ[:, 192:256], op=ALU.mult), r=["LQ"], w=["LP"])
            T.op("dve", nc.vector.tensor_reduce, KW(out=LS[:, 0:1], in_=LP[:, 0:64], axis=AX.X, op=ALU.add), r=["LP"], w=["LS"])
            T.op("dve", nc.vector.tensor_reduce, KW(out=LS[:, 1:2], in_=LP[:, 64:128], axis=AX.X, op=ALU.add), r=["LP"], w=["LS"])
            T.op("act", nc.scalar.activation, KW(out=LS[:, 2:4], in_=LS[:, 0:2], func=AF.Exp), r=["LS"], w=["LS"])
            T.op("dve", nc.vector.tensor_tensor, KW(out=LAMT[:, 0:1], in0=LS[:, 2:3], in1=LS[:, 3:4], op=ALU.subtract), r=["LS"], w=["LAMT"])
            T.op("dve", nc.vector.tensor_scalar, KW(out=LAMT[:, 0:1], in0=LAMT[:, 0:1], scalar1=LAM_INIT, scalar2=None, op0=ALU.add), r=["LAMT"], w=["LAMT"])
            T.op("dve", nc.vector.tensor_scalar, KW(out=LAMT[:, 1:2], in0=LAMT[:, 0:1], scalar1=-1.0, scalar2=None, op0=ALU.mult), r=["LAMT"], w=["LAMT"])
            T.op("dve", nc.vector.tensor_scalar, KW(out=LAMT[:, 2:3], in0=CF[:, C_GSUB:C_GSUB + 1], scalar1=1.0 - LAM_INIT, scalar2=None, op0=ALU.mult), r=["CF", "LAMT"], w=["LAMT"])
            lbl = CF[:, C_LBL:C_LBL + 32].rearrange("p (a s h) -> p a s h", a=2, s=2)
            T.op("dve", nc.vector.tensor_tensor, KW(out=LBV[:, 0:16].rearrange("p (a h) -> p a h", a=2), in0=lbl[:, :, 0, :], in1=lbl[:, :, 1, :], op=ALU.subtract), r=["CF"], w=["LBV"])
            T.op("act", nc.scalar.activation, KW(out=LBV[:, 0:16], in_=LBV[:, 0:16], func=AF.Sigmoid), r=["LBV"], w=["LBV"])
            T.op("dve", nc.vector.tensor_scalar, KW(out=LBV[:, 16:32], in0=LBV[:, 0:16], scalar1=-1.0, scalar2=1.0, op0=ALU.mult, op1=ALU.add), r=["LBV"], w=["LBV"])
            for pc in range(3):
                w_ = 512 if pc < 2 else 256
                T.op("pe", nc.tensor.matmul, KW(PS[0][0:8, 0:w_], lhsT=RB[:, :], rhs=EH[:, pc * 512:pc * 512 + w_], start=True, stop=True), r=["RB", "EH"], w=["ps0"])
                T.op("dve", nc.vector.tensor_copy, KW(out=TB[:, pc * 512:pc * 512 + w_], in_=PS[0][0:8, 0:w_]), r=["ps0"], w=["TB"])
            T.dma("sp", s_tb0[:, :], TB[:], "TB", r=["TB"], w=["tb0"])
            T.dma("sp", s_rep[:, :, :], bass.AP(tensor=s_tb0.tensor, offset=0, ap=[[1280, 8], [0, 129], [1, 1280]]), "REP", r=["tb0"], w=["rep"])
            T.barrier()

        def norm_T(ph_bufs, src_ap, src_tok, gcol, dst, dst_tok, c0, rtok=()):
            SS, XN = ph_bufs
            ss, sst = SS.next()
            xn, xnt = XN.next()
            if OPT_ACC:
                T.op("act", nc.scalar.activation, KW(out=xn[:], in_=src_ap, func=AF.Square, accum_out=ss[:, 0:1]), r=[src_tok, *rtok], w=[xnt, sst])
            else:
                T.op("act", nc.scalar.activation, KW(out=xn[:], in_=src_ap, func=AF.Square), r=[src_tok, *rtok], w=[xnt])
                T.op("dve", nc.vector.tensor_reduce, KW(out=ss[:, 0:1], in_=xn[:], axis=AX.X, op=ALU.add), r=[xnt], w=[sst])
            T.op("act", nc.scalar.activation, KW(out=ss[:, 1:2], in_=ss[:, 0:1], func=AF.Ln, bias=EPSC, scale=1.0 / D), r=[sst], w=[sst])
            T.op("act", nc.scalar.activation, KW(out=ss[:, 2:3], in_=ss[:, 1:2], func=AF.Exp, scale=-0.5), r=[sst], w=[sst])
            T.op("act", nc.scalar.activation, KW(out=xn[:], in_=src_ap, func=AF.Copy, scale=ss[:, 2:3]), r=[src_tok, sst], w=[xnt])
            for q4 in range(4):
                reg = PTR.i % 2
                PTR.i += 1
                ptt = ("PT", reg)
                for j in range(4):
                    kc = q4 * 4 + j
                    T.op("pe", nc.tensor.transpose, KW(out=PT[reg][:, j * 128:(j + 1) * 128], in_=xn[:, kc * 128:(kc + 1) * 128], identity=IDB), r=[xnt], w=[ptt])
                gv = CF[:, gcol + q4 * 4: gcol + q4 * 4 + 4]
                T.op("dve", nc.vector.tensor_tensor, KW(
                    out=dst[:, q4 * 4:(q4 + 1) * 4, c0:c0 + 128],
                    in0=PT[reg][:, 0:512].rearrange("p (a b) -> p a b", a=4),
                    in1=gv.unsqueeze(2).broadcast_to([128, 4, 128]), op=ALU.mult), r=[ptt], w=[dst_tok])
            return ss

        class _C:
            pass
        PTR = _C()
        PTR.i = 0

        def load_panel(WP, name, cp, kg, g_n=PG):
            wb, wt = WP.next()
            T.dma("sp", wb[:, 0:g_n, :], wpan[name][cp, kg, :, 0:g_n, :], "wp%d" % wt[1], w=[wt])
            return wb, wt

        lo = 0
        ro = NLOC
        for ji, (L, R) in enumerate(jobs):
            TT = L + R
            NB = TT // 128
            with ExitStack() as ph:
                XS = Rot("XS", [sb(ph, "p1xs%d" % i, [128, D], F32) for i in range(2)])
                SS = Rot("SS", [sb(ph, "p1ss%d" % i, [128, 4], F32) for i in range(2)])
                XN = Rot("XN", [sb(ph, "p1xn%d" % i, [128, D], BF16) for i in range(2)])
                HT = Rot("HT", [sb(ph, "p1ht%d" % i, [128, KC, 512], BF16) for i in range(2)])
                WP = Rot("WP", [sb(ph, "p1wp%d" % i, [128, PG, 512], BF16) for i in range(4)])
                STB = Rot("STB", [sb(ph, "p1sb%d" % i, [128, 512], BF16) for i in range(6)])
                STF = Rot("STF", [sb(ph, "p1sf%d" % i, [128, 512], F32) for i in range(4)])
                PSR = Rot("PS", PS)
                tiles = [(lo + t * 512, t * 512, True) for t in range(L // 512)] + \
                        [(ro + t * 512, L + t * 512, False) for t in range(R // 512)]
                def p1_norm(row0):
                    ht, htt = HT.next()
                    for sub in range(4):
                        xb, xt = XS.next()
                        T.dma("pool", xb[:], xs[row0 + sub * 128: row0 + (sub + 1) * 128, :], "xs%d" % xt[1], w=[xt])
                        norm_T((SS, XN), xb[:], xt, C_GPM, ht, htt, sub * 128)
                    return ht, htt
                nxt = p1_norm(tiles[0][0])
                for ti, (row0, pos0, is_loc) in enumerate(tiles):
                    ht, htt = nxt
                    if ti + 1 < len(tiles):
                        nxt = p1_norm(tiles[ti + 1][0])
                    cps = list(range(24)) if is_loc else [2, 3, 4, 5, 10, 11, 12, 13]
                    for cp in cps:
                        cat = cp // 2 if cp < 16 else 8
                        tokmajor = cat in (2, 6)
                        banks = [PSR.next() for _ in range(4)]
                        for kg in range(2):
                            wb, wt = load_panel(WP, "w_in", cp, kg)
                            for j in range(4):
                                pb, pbt = banks[j]
                                for g in range(PG):
                                    kc = kg * PG + g
                                    st, sp_ = (kc == 0), (kc == KC - 1)
                                    if tokmajor:
                                        T.op("pe", nc.tensor.matmul, KW(pb[:, :], lhsT=ht[:, kc, j * 128:(j + 1) * 128], rhs=wb[:, g, :], start=st, stop=sp_), r=[htt, wt], w=[pbt])
                                    else:
                                        T.op("pe", nc.tensor.matmul, KW(pb[:, :], lhsT=wb[:, g, j * 128:(j + 1) * 128], rhs=ht[:, kc, :], start=st, stop=sp_), r=[htt, wt], w=[pbt])
                        for j in range(4):
                            pb, pbt = banks[j]
                            ch = (cp % 2) * 4 + j
                            if cat in (4, 5):
                                stg, stt = STF.next()
                            else:
                                stg, stt = STB.next()
                            key = "st%s%d" % (stt[0], stt[1])
                            if cat == 0:
                                T.op("dve", nc.vector.tensor_scalar, KW(out=stg[:], in0=pb[:, :], scalar1=0.125, scalar2=None, op0=ALU.mult), r=[pbt], w=[stt])
                                T.dma("pool", s_qa[ch, :, pos0:pos0 + 512], stg[:], key, r=[stt], w=[("qa", ch)])
                            elif cat == 1:
                                T.op("dve", nc.vector.tensor_copy, KW(out=stg[:], in_=pb[:, :]), r=[pbt], w=[stt])
                                T.dma("pool", s_ka[ch, :, pos0:pos0 + 512], stg[:], key, r=[stt], w=[("ka", ch)])
                            elif cat in (2, 6):
                                dst = s_va if cat == 2 else s_vb
                                h0 = (cp % 2) * 4
                                blk = pos0 // 128 + j
                                T.op("act", nc.scalar.copy, KW(out=stg[:], in_=pb[:, :]), r=[pbt], w=[stt])
                                d_ap = dst[h0:h0 + 4, :, blk, :].rearrange("h p e -> p h e")
                                T.dma("pool", d_ap, stg[:].rearrange("p (h e) -> p h e", h=4), key, r=[stt], w=[("v", cat, h0)])
                            elif cat == 3:
                                T.op("dve", nc.vector.tensor_scalar, KW(out=stg[:], in0=pb[:, :], scalar1=128 ** -0.5, scalar2=None, op0=ALU.mult), r=[pbt], w=[stt])
                                T.dma("pool", s_qb[ch, :, pos0:pos0 + 512], stg[:], key, r=[stt], w=[("qb", ch)])
                            elif cat == 4:
                                T.op("act", nc.scalar.copy, KW(out=stg[:], in_=pb[:, :]), r=[pbt], w=[stt])
                                T.dma("pool", s_zf[ch, :, pos0:pos0 + 512], stg[:], key, r=[stt], w=[("zf", ch)])
                            elif cat == 5:
                                T.op("dve", nc.vector.tensor_copy, KW(out=stg[:], in_=pb[:, :]), r=[pbt], w=[stt])
                                T.dma("pool", s_zb[ch, :, pos0:pos0 + 512], stg[:], key, r=[stt], w=[("zb", ch)])
                            elif cat == 7:
                                T.op("act", nc.scalar.activation, KW(out=stg[:], in_=pb[:, :], func=AF.Silu), r=[pbt], w=[stt])
                                T.dma("pool", s_sg[ch, :, pos0:pos0 + 512], stg[:], key, r=[stt], w=[("sg", ch)])
                            else:
                                gc = (cp - 16) * 4 + j
                                T.op("act", nc.scalar.activation, KW(out=stg[:], in_=pb[:, :], func=AF.Sigmoid, bias=CF[:, C_BM + gc:C_BM + gc + 1]), r=[pbt], w=[stt])
                                T.dma("pool", s_gt[gc, :, pos0:pos0 + 512], stg[:], key, r=[stt], w=[("gt", gc)])
                T.barrier()

            if ji == 0:
                convert(tuple(n for n, _, _ in WSPECS if n != "w_in"), "cv2")
            with ExitStack() as ph:
                KT = Rot("KT", [sb(ph, "p2kt%d" % i, [128, TT], BF16) for i in range(2)])
                VV = Rot("VV", [sb(ph, "p2vv%d" % i, [128, NB, 128], BF16) for i in range(2)])
                QT = Rot("QT", [sb(ph, "p2qt%d" % i, [128, L], BF16) for i in range(2)])
                BT = Rot("BT", [sb(ph, "p2bt%d" % i, [128, 6, 512], F32) for i in range(2)])
                PP = Rot("PP", [sb(ph, "p2pp%d" % i, [128, 512], BF16) for i in range(6)])
                SBT = Rot("SBT", [sb(ph, "p2sbt%d" % i, [128, 512], F32) for i in range(3)])
                FZ = [sb(ph, "p2fz%d" % i, [128, 512], F32) for i in range(5)]
                ACC = [sb(ph, "p2acc%d" % i, [128, 512], F32) for i in range(2)]
                ONESF = sb(ph, "p2onesf", [128, 128], F32)
                T.op("pool", nc.gpsimd.memset, KW(ONESF[:], 1.0), w=["ONESF"])
                SQ = sb(ph, "p2sq", [128, 512], BF16)
                YS = Rot("YS", [sb(ph, "p2ys%d" % i, [128, 512], BF16) for i in range(2)])
                if OPT_PTF:
                    SR = Rot("SR", [PS[0], PS[1], PT[0][:, :].bitcast(F32), PT[1][:, :].bitcast(F32)])
                    LA = 4
                else:
                    SR = Rot("SR", [PS[0], PS[1]])
                    LA = 2
                for h in range(8):
                    kt, ktt = KT.next()
                    vv, vvt = VV.next()
                    qt, qtt = QT.next()
                    bt, btt = BT.next()
                    T.dma("sp", kt[:], s_ka[h, :, 0:TT], "kt%d" % ktt[1], w=[ktt])
                    T.dma("sp", vv[:], s_va[h, :, 0:NB, :], "vv%d" % vvt[1], w=[vvt])
                    T.dma("sp", qt[:], s_qa[h, :, 0:L], "qt%d" % qtt[1], w=[qtt])
                    for oi in range(6):
                        off = -128 + oi * 128
                        T.dma("sp", bt[:, oi, :], bass.AP(tensor=s_rep.tensor, offset=h * 129 * 1280 + 640 - off, ap=[[1279, 128], [1, 512]]), "bt%d" % btt[1], w=[btt])
                    for qb in range(L // 512):
                        q0 = qb * 512
                        steps = [(c, kb) for kb in range(NB) for c in range(2)]
                        pend = {}

                        def qk_step(i, q0=q0, kt=kt, qt=qt, bt=bt, h=h, pend=pend, ktt=ktt, qtt=qtt, btt=btt):
                            c, kb = steps[i]
                            off = kb * 128 - q0
                            sbk, sbt_ = SR.next()
                            pp, ppt = PP.next()
                            T.op("pe", nc.tensor.matmul, KW(sbk[:, :], lhsT=kt[c * 64:(c + 1) * 64, kb * 128:(kb + 1) * 128], rhs=qt[c * 64:(c + 1) * 64, q0:q0 + 512], start=True, stop=True), r=[ktt, qtt], w=[sbt_])
                            if -128 <= off <= 512:
                                oi = (off + 128) // 128
                                tb_, tbt = SBT.next()
                                T.op("dve", nc.vector.tensor_tensor, KW(out=tb_[:], in0=sbk[:, :], in1=bt[:, oi, :], op=ALU.add), r=[sbt_, btt], w=[tbt])
                                T.op("act", nc.scalar.activation, KW(out=pp[:], in_=tb_[:], func=AF.Exp), r=[tbt], w=[ppt])
                            else:
                                fcol = h if off < 0 else 8 + h
                                T.op("act", nc.scalar.activation, KW(out=pp[:], in_=sbk[:, :], func=AF.Exp, bias=FARB[:, fcol:fcol + 1]), r=[sbt_], w=[ppt])
                            if kb == 0:
                                T.op("dve", nc.vector.tensor_copy, KW(out=ACC[c][:], in_=pp[:]), r=[ppt], w=[("acc", c)])
                            else:
                                T.op("dve", nc.vector.tensor_tensor, KW(out=ACC[c][:], in0=ACC[c][:], in1=pp[:], op=ALU.add), r=[ppt, ("acc", c)], w=[("acc", c)])
                            pend[i] = (pp, ppt)

                        def pv_step(i, vv=vv, vvt=vvt, pend=pend):
                            c, kb = steps[i]
                            pp, ppt = pend.pop(i)
                            st, sp_ = (kb == 0), (kb == NB - 1)
                            T.op("pe", nc.tensor.matmul, KW(PS[2 + c][:, :], lhsT=vv[:, kb, :], rhs=pp[:], start=st, stop=sp_), r=[vvt, ppt], w=[("psO", c)])
                        for i in range(min(LA, len(steps))):
                            qk_step(i)
                        for i in range(0, len(steps), 2):
                            pv_step(i)
                            pv_step(i + 1)
                            for k2 in (i + LA, i + LA + 1):
                                if k2 < len(steps):
                                    qk_step(k2)
                        for c in range(2):
                            T.op("pe", nc.tensor.matmul, KW(PS[4 + c][:, :], lhsT=ONESF[:], rhs=ACC[c][:], start=True, stop=True), r=[("acc", c), "ONESF"], w=[("psZ", c)])
                        T.op("dve", nc.vector.reciprocal, KW(out=FZ[0][:], in_=PS[4][:, :]), r=[("psZ", 0)], w=["fz0"])
                        T.op("dve", nc.vector.reciprocal, KW(out=FZ[1][:], in_=PS[5][:, :]), r=[("psZ", 1)], w=["fz1"])
                        T.op("dve", nc.vector.tensor_tensor, KW(out=FZ[2][:], in0=PS[2][:, :], in1=FZ[0][:], op=ALU.mult), r=[("psO", 0), "fz0"], w=["fz2"])
                        T.op("dve", nc.vector.tensor_tensor, KW(out=FZ[3][:], in0=PS[3][:, :], in1=FZ[1][:], op=ALU.mult), r=[("psO", 1), "fz1"], w=["fz3"])
                        T.op("dve", nc.vector.scalar_tensor_tensor, KW(out=FZ[4][:], in0=FZ[3][:], scalar=LAMT[:, 1:2], in1=FZ[2][:], op0=ALU.mult, op1=ALU.add), r=["fz2", "fz3"], w=["fz4"])
                        T.op("act", nc.scalar.activation, KW(out=SQ[:], in_=FZ[4][:], func=AF.Square), r=["fz4"], w=["sq"])
                        mbk, mbt = SR.next()
                        T.op("pe", nc.tensor.matmul, KW(mbk[:, :], lhsT=ONES, rhs=SQ[:], start=True, stop=True), r=["sq"], w=[mbt])
                        T.op("act", nc.scalar.activation, KW(out=FZ[0][:], in_=mbk[:, :], func=AF.Ln, bias=EPSC, scale=1.0 / 128), r=[mbt], w=["fz0"])
                        T.op("act", nc.scalar.activation, KW(out=FZ[0][:], in_=FZ[0][:], func=AF.Exp, scale=-0.5), r=["fz0"], w=["fz0"])
                        ys, yst = YS.next()
                        T.op("dve", nc.vector.scalar_tensor_tensor, KW(out=ys[:], in0=FZ[4][:], scalar=LAMT[:, 2:3], in1=FZ[0][:], op0=ALU.mult, op1=ALU.mult), r=["fz4", "fz0"], w=[yst])
                        T.dma("pool", s_ya[h, :, q0:q0 + 512], ys[:], "ys%d" % yst[1], r=[yst], w=[("ya", h)])
                T.barrier()

            with ExitStack() as ph:
                PW = min(2048, L)
                NCH = PW // 64
                NBK = PW // 128
                CM = sb(ph, "p3cm", [128, PW], F32)
                A = sb(ph, "p3A", [128, PW], F32)
                KK = sb(ph, "p3KK", [128, PW], F32)
                C = sb(ph, "p3C", [128, PW], F32)
                E4 = sb(ph, "p3E4", [128, PW], F32)
                D1 = sb(ph, "p3D1", [128, PW], F32)
                EX = sb(ph, "p3EX", [128, PW], F32)
                DEC = sb(ph, "p3DEC", [128, NCH], F32)
                Q = sb(ph, "p3Q", [128, PW], BF16)
                SG = sb(ph, "p3SG", [128, PW], BF16)
                QTt = sb(ph, "p3QT", [128, PW], BF16)
                KTt = sb(ph, "p3KT", [128, PW], BF16)
                KH = sb(ph, "p3KH", [128, PW], BF16)
                QE = sb(ph, "p3QE", [128, PW], BF16)
                QT2 = sb(ph, "p3QT2", [128, PW], BF16)
                KT2 = sb(ph, "p3KT2", [128, PW], BF16)
                QE2 = sb(ph, "p3QE2", [128, PW], BF16)
                VB = sb(ph, "p3VB", [128, NBK, 128], BF16)
                SPV = sb(ph, "p3SPV", [128, L // 64, 128], BF16)
                SF = Rot("SF", [sb(ph, "p3sf%d" % i, [128, 128], F32) for i in range(2)])
                SFA = sb(ph, "p3SFA", [128, NCH, 128], BF16)
                AMA = sb(ph, "p3AMA", [128, NBK, 2, 128], BF16)
                KHT = Rot("KHT", [sb(ph, "p3kht%d" % i, [128, 128], BF16) for i in range(3)])
                OS = sb(ph, "p3OS", [128, 512], F32)
                RS = sb(ph, "p3RS", [128, 512], F32)
                Y1 = sb(ph, "p3Y1", [128, 512], F32)
                SQ = sb(ph, "p3sq", [128, 512], BF16)
                YS = Rot("YS2", [sb(ph, "p3ys%d" % i, [128, 512], BF16) for i in range(2)])
                T.dma("sp", CM[:], cmaskd[:, 0:PW], "CM", w=["CM"])
                NH = 2 if PW >= 1024 else 1
                HW_ = PW // NH
                NCH2 = HW_ // 64
                hs = lambda t, hf: t[:, hf * HW_:(hf + 1) * HW_]
                c3 = lambda t, hf: t[:, hf * HW_:(hf + 1) * HW_].rearrange("p (c j) -> p c j", j=64)
                hb_of = lambda b: (b * 128) // HW_

                def halves(fn):
                    a0 = len(T.ops)
                    fn(0)
                    if NH == 1:
                        return
                    a1 = len(T.ops)
                    fn(1)
                    a2 = len(T.ops)
                    assert a1 - a0 == a2 - a1
                    A_, B_ = T.ops[a0:a1], T.ops[a1:a2]
                    T.ops[a0:a2] = [x for pair in zip(A_, B_) for x in pair]

                def gate_math(h, d, zsrc, p0, hf):
                    Ah, tA, tK, tC = hs(A, hf), ("A", hf), ("KK", hf), ("C", hf)
                    T.dma("sp", Ah, zsrc[h, :, p0 + hf * HW_:p0 + (hf + 1) * HW_], "A%d" % hf, w=[tA])
                    T.op("act", nc.scalar.activation, KW(out=Ah, in_=Ah, func=AF.Sigmoid, scale=-1.0), r=[tA], w=[tA])
                    T.op("act", nc.scalar.activation, KW(out=hs(KK, hf), in_=Ah, func=AF.Copy, scale=LBV[:, 16 + d * 8 + h:17 + d * 8 + h]), r=[tA], w=[tK])
                    T.op("act", nc.scalar.activation, KW(out=Ah, in_=hs(KK, hf), func=AF.Ln, bias=ONE1[:, 0:1], scale=-1.0), r=[tK], w=[tA])
                    T.op("dve", nc.vector.tensor_tensor_scan, KW(out=hs(C, hf), data0=hs(CM, hf), data1=Ah, initial=0.0, op0=ALU.mult, op1=ALU.add), r=[tA, "CM"], w=[tC])

                def ew_sweepA(h, p0, hf):
                    gate_math(h, 1, s_zb, p0, hf)
                    tA, tK, tC, tE = ("A", hf), ("KK", hf), ("C", hf), ("EX", hf)
                    T.op("dve", nc.vector.tensor_tensor, KW(out=hs(EX, hf), in0=hs(C, hf), in1=hs(A, hf), op=ALU.subtract), r=[tC, tA], w=[tE])
                    T.op("act", nc.scalar.activation, KW(out=hs(EX, hf), in_=hs(EX, hf), func=AF.Exp), r=[tE], w=[tE])
                    T.op("dve", nc.vector.tensor_tensor, KW(out=hs(KH, hf), in0=hs(KK, hf), in1=hs(EX, hf), op=ALU.mult), r=[tK, tE], w=[("KH", hf)])
                    T.op("act", nc.scalar.activation, KW(out=DEC[:, hf * NCH2:(hf + 1) * NCH2], in_=c3(C, hf)[:, :, 63], func=AF.Exp), r=[tC], w=[("DEC", hf)])

                def ew_sweepB(h, p0, hf):
                    tA, tK, tC, tE, tD, t4 = ("A", hf), ("KK", hf), ("C", hf), ("EX", hf), ("D1", hf), ("E4", hf)
                    tQ = ("Q", hf)
                    Qh, KKh, E4h, D1h = hs(Q, hf), hs(KK, hf), hs(E4, hf), hs(D1, hf)
                    bc = lambda t, col: c3(t, hf)[:, :, col:col + 1].broadcast_to([128, NCH2, 64])
                    T.dma("sp", Qh, s_qb[h, :, p0 + hf * HW_:p0 + (hf + 1) * HW_], "Q%d" % hf, w=[tQ])
                    T.dma("sp", hs(SG, hf), s_sg[h, :, p0 + hf * HW_:p0 + (hf + 1) * HW_], "SG%d" % hf, w=[("SG", hf)])
                    gate_math(h, 1, s_zb, p0, hf)
                    T.op("dve", nc.vector.tensor_tensor, KW(out=hs(EX, hf), in0=hs(C, hf), in1=hs(A, hf), op=ALU.subtract), r=[tC, tA], w=[tE])
                    T.op("dve", nc.vector.tensor_tensor, KW(out=c3(D1, hf), in0=c3(EX, hf), in1=bc(C, 31), op=ALU.subtract), r=[tE, tC], w=[tD])
                    T.op("act", nc.scalar.activation, KW(out=E4h, in_=D1h, func=AF.Exp), r=[tD], w=[t4])
                    T.op("dve", nc.vector.tensor_tensor, KW(out=hs(KT2, hf), in0=KKh, in1=E4h, op=ALU.mult), r=[tK, t4], w=[("KT2", hf)])
                    T.op("act", nc.scalar.activation, KW(out=E4h, in_=D1h, func=AF.Exp, scale=-1.0), r=[tD], w=[t4])
                    T.op("dve", nc.vector.tensor_tensor, KW(out=hs(QT2, hf), in0=Qh, in1=E4h, op=ALU.mult), r=[tQ, t4], w=[("QT2", hf)])
                    T.op("dve", nc.vector.tensor_tensor, KW(out=c3(D1, hf), in0=c3(EX, hf), in1=bc(C, 63), op=ALU.subtract), r=[tE, tC], w=[tD])
                    T.op("act", nc.scalar.activation, KW(out=E4h, in_=D1h, func=AF.Exp, scale=-1.0), r=[tD], w=[t4])
                    T.op("dve", nc.vector.tensor_tensor, KW(out=hs(QE2, hf), in0=Qh, in1=E4h, op=ALU.mult), r=[tQ, t4], w=[("QE2", hf)])
                    gate_math(h, 0, s_zf, p0, hf)
                    T.op("dve", nc.vector.tensor_tensor, KW(out=c3(D1, hf), in0=c3(C, hf), in1=bc(C, 31), op=ALU.subtract), r=[tC], w=[tD])
                    T.op("act", nc.scalar.activation, KW(out=E4h, in_=D1h, func=AF.Exp), r=[tD], w=[t4])
                    T.op("dve", nc.vector.tensor_tensor, KW(out=hs(QTt, hf), in0=Qh, in1=E4h, op=ALU.mult), r=[tQ, t4], w=[("QT", hf)])
                    T.op("act", nc.scalar.activation, KW(out=E4h, in_=D1h, func=AF.Exp, scale=-1.0), r=[tD], w=[t4])
                    T.op("dve", nc.vector.tensor_tensor, KW(out=hs(KTt, hf), in0=KKh, in1=E4h, op=ALU.mult), r=[tK, t4], w=[("KT", hf)])
                    T.op("dve", nc.vector.tensor_tensor, KW(out=c3(D1, hf), in0=c3(C, hf), in1=bc(C, 63), op=ALU.subtract), r=[tC], w=[tD])
                    T.op("act", nc.scalar.activation, KW(out=E4h, in_=D1h, func=AF.Exp, scale=-1.0), r=[tD], w=[t4])
                    T.op("dve", nc.vector.tensor_tensor, KW(out=hs(KH, hf), in0=KKh, in1=E4h, op=ALU.mult), r=[tK, t4], w=[("KH", hf)])
                    T.op("act", nc.scalar.activation, KW(out=E4h, in_=hs(C, hf), func=AF.Exp), r=[tC], w=[t4])
                    T.op("dve", nc.vector.tensor_tensor, KW(out=hs(QE, hf), in0=Qh, in1=E4h, op=ALU.mult), r=[tQ, t4], w=[("QE", hf)])
                    T.op("dve", nc.vector.tensor_copy, KW(out=DEC[:, hf * NCH2:(hf + 1) * NCH2], in_=c3(E4, hf)[:, :, 63]), r=[t4], w=[("DEC", hf)])

                def khT_block(b):
                    reg = PTR.i % 2
                    PTR.i += 1
                    ptt = ("PT", reg)
                    kh, kht = KHT.next()
                    T.op("pe", nc.tensor.transpose, KW(out=PT[reg][:, 0:128], in_=KH[:, b * 128:(b + 1) * 128], identity=IDB), r=[("KH", hb_of(b))], w=[ptt])
                    T.op("act", nc.scalar.copy, KW(out=kh[:], in_=PT[reg][:, 0:128]), r=[ptt], w=[kht])
                    return kh, kht

                KVR = _C()
                KVR.i = 0

                def state_step(kh, kht, b, part, chunk_local, scur, scurt):
                    reg = KVR.i % 2
                    KVR.i += 1
                    kvt = ("kv", reg)
                    p0_, p1_ = part * 64, part * 64 + 64
                    T.op("pe", nc.tensor.matmul, KW(PS[reg][:, 0:128], lhsT=kh[p0_:p1_, :], rhs=VB[p0_:p1_, b, :], start=True, stop=True), r=[kht, "VB"], w=[kvt])
                    snew, snewt = SF.next()
                    T.op("dve", nc.vector.scalar_tensor_tensor, KW(out=snew[:], in0=scur[:], scalar=DEC[:, chunk_local:chunk_local + 1], in1=PS[reg][:, 0:128], op0=ALU.mult, op1=ALU.add), r=[scurt, ("DEC", hb_of(b)), kvt], w=[snewt])
                    return snew, snewt

                for h in range(8):
                    scur, scurt = SF.next()
                    T.op("pool", nc.gpsimd.memset, KW(scur[:], 0.0), w=[scurt])
                    if R == 0:
                        T.op("pool", nc.gpsimd.memset, KW(SPV[:, L // 64 - 1, :], 0.0), w=[("SPV", L // 64 - 1)])
                    for pi in reversed(range(TT // PW)):
                        p0 = pi * PW
                        T.dma("sp", VB[:], s_vb[h, :, p0 // 128:p0 // 128 + NBK, :], "VB", w=["VB"])
                        halves(lambda hf, h=h, p0=p0: ew_sweepA(h, p0, hf))
                        order = list(reversed(range(NBK)))
                        khs = {order[0]: khT_block(order[0])}
                        for idx, b in enumerate(order):
                            if idx + 1 < NBK:
                                khs[order[idx + 1]] = khT_block(order[idx + 1])
                            kh, kht = khs.pop(b)
                            for part in (1, 0):
                                cl = 2 * b + part
                                cg = p0 // 64 + cl
                                scur, scurt = state_step(kh, kht, b, part, cl, scur, scurt)
                                if 1 <= cg <= L // 64:
                                    T.op("act", nc.scalar.copy, KW(out=SPV[:, cg - 1, :], in_=scur[:]), r=[scurt], w=[("SPV", cg - 1)])
                    scur, scurt = SF.next()
                    T.op("pool", nc.gpsimd.memset, KW(scur[:], 0.0), w=[scurt])
                    for pi in range(L // PW):
                        p0 = pi * PW
                        T.dma("sp", VB[:], s_vb[h, :, p0 // 128:p0 // 128 + NBK, :], "VB", w=["VB"])
                        halves(lambda hf, h=h, p0=p0: ew_sweepB(h, p0, hf))
                        for b in range(NBK):
                            bs = slice(b * 128, (b + 1) * 128)
                            for dirn, (kt_, qt_, msk, kn, qn) in enumerate(((KTt, QTt, MSKF, "KT", "QT"), (KT2, QT2, MSKB, "KT2", "QT2"))):
                                at = ("aT", dirn)
                                T.op("pe", nc.tensor.matmul, KW(PS[2 + dirn][:, 0:128], lhsT=kt_[:, bs], rhs=qt_[:, bs], start=True, stop=True), r=[(kn, hb_of(b)), (qn, hb_of(b))], w=[at])
                                T.op("dve", nc.vector.tensor_tensor, KW(out=AMA[:, b, dirn, :], in0=PS[2 + dirn][:, 0:128], in1=msk, op=ALU.mult), r=[at], w=[("AMA", b)])
                        khs = {0: khT_block(0)}
                        for b in range(NBK):
                            if b + 1 < NBK:
                                khs[b + 1] = khT_block(b + 1)
                            kh, kht = khs.pop(b)
                            for part in (0, 1):
                                cl = 2 * b + part
                                T.op("act", nc.scalar.copy, KW(out=SFA[:, cl, :], in_=scur[:]), r=[scurt], w=[("SFA", cl)])
                                scur, scurt = state_step(kh, kht, b, part, cl, scur, scurt)
                        pendg = None

                        def finish_group(g):
                            t0_, ls = g
                            T.op("pe", nc.tensor.matmul, KW(PS[5][:, :], lhsT=ONES, rhs=SQ[:], start=True, stop=True), r=["sq"], w=["psM"])
                            T.op("act", nc.scalar.activation, KW(out=RS[:], in_=PS[5][:, :], func=AF.Ln, bias=EPSC, scale=1.0 / 128), r=["psM"], w=["RS"])
                            T.op("act", nc.scalar.activation, KW(out=RS[:], in_=RS[:], func=AF.Exp, scale=-0.5), r=["RS"], w=["RS"])
                            T.op("dve", nc.vector.scalar_tensor_tensor, KW(out=Y1[:], in0=OS[:], scalar=CF[:, C_GHN:C_GHN + 1], in1=RS[:], op0=ALU.mult, op1=ALU.mult), r=["OS", "RS"], w=["Y1"])
                            ys, yst = YS.next()
                            T.op("dve", nc.vector.tensor_tensor, KW(out=ys[:], in0=Y1[:], in1=SG[:, ls], op=ALU.mult), r=["Y1", ("SG", ls.start // HW_)], w=[yst])
                            T.dma("pool", s_yb[h, :, t0_:t0_ + 512], ys[:], "yt%d" % yst[1], r=[yst], w=[("yb", h)])
                        for b in range(NBK):
                            col = (b % 4) * 128
                            obank = PS[4]
                            obt = ("psOh", 0)
                            T.op("pe", nc.tensor.matmul, KW(obank[:, col:col + 128], lhsT=VB[:, b, :], rhs=AMA[:, b, 0, :], start=True, stop=False), r=["VB", ("AMA", b)], w=[obt])
                            T.op("pe", nc.tensor.matmul, KW(obank[:, col:col + 128], lhsT=VB[:, b, :], rhs=AMA[:, b, 1, :], start=False, stop=False), r=["VB", ("AMA", b)], w=[obt])
                            for part in (0, 1):
                                cl = 2 * b + part
                                cg = p0 // 64 + cl
                                cs = slice(cl * 64, cl * 64 + 64)
                                oc = slice(col + part * 64, col + part * 64 + 64)
                                T.op("pe", nc.tensor.matmul, KW(obank[:, oc], lhsT=SFA[:, cl, :], rhs=QE[:, cs], start=False, stop=False), r=[("SFA", cl), ("QE", hb_of(b))], w=[obt])
                                T.op("pe", nc.tensor.matmul, KW(obank[:, oc], lhsT=SPV[:, cg, :], rhs=QE2[:, cs], start=False, stop=(part == 1)), r=[("SPV", cg), ("QE2", hb_of(b))], w=[obt])
                            if b % 4 == 3:
                                if pendg is not None:
                                    finish_group(pendg)
                                T.op("dve", nc.vector.tensor_copy, KW(out=OS[:], in_=obank[:, :]), r=[obt], w=["OS"])
                                T.op("act", nc.scalar.activation, KW(out=SQ[:], in_=OS[:], func=AF.Square), r=["OS"], w=["sq"])
                                pendg = (p0 + (b - 3) * 128, slice((b - 3) * 128, (b + 1) * 128))
                        finish_group(pendg)
                T.barrier()

            with ExitStack() as ph:
                X = sb(ph, "p4X", [128, 4, D], F32)
                U = sb(ph, "p4U", [128, 4, D], F32)
                HTb = sb(ph, "p4HT", [128, KC, 512], BF16)
                SCR = sb(ph, "p4SCR", [128, 24, 512], BF16)
                GAB = Rot("GAB", [sb(ph, "p4gab%d" % i, [128, 2, 512], BF16) for i in range(8)])
                QX = sb(ph, "p4QX", [128, 4, 512], BF16)
                OX = sb(ph, "p4OX", [128, 4, 512], BF16)
                KXT = sb(ph, "p4KXT", [128, 4, NMEM], BF16)
                VX = sb(ph, "p4VX", [128, 2, 512], BF16)
                WP = Rot("WP", [sb(ph, "p4wp%d" % i, [128, PG, 512], BF16) for i in range(3)])
                GP = Rot("GP", [sb(ph, "p4gp", [128, D], F32)])
                SSP = sb(ph, "p4ssp", [128, 16], F32)
                JKo = None if OPT_PN else sb(ph, "p4jko", [128, D], F32)
                JQ = Rot("JQ", [sb(ph, "p4jq%d" % i, [128, 512], BF16) for i in range(2)])
                SS = Rot("SS", [sb(ph, "p4ss%d" % i, [128, 4], F32) for i in range(2)])
                XN = Rot("XN", [sb(ph, "p4xn", [128, D], BF16)])
                PP = Rot("PP", [sb(ph, "p4pp%d" % i, [128, 512], BF16) for i in range(2)])
                RZ = sb(ph, "p4rz", [128, 512], F32)
                SGt = Rot("SGt", [sb(ph, "p4sg%d" % i, [128, 512], F32) for i in range(4)])
                PSR = Rot("PS", PS)

                def gemm_tok(name, act, acttok, nkc, ncp, evac, kc_off=0, kg_list=None):
                    nkg = -(-nkc // PG)
                    for cp in range(ncp):
                        banks = [PSR.next() for _ in range(4)]
                        for kg in range(nkg):
                            g_n = min(PG, nkc - kg * PG)
                            wb, wt = load_panel(WP, name, cp, kg + kc_off // PG, g_n)
                            for sub in range(4):
                                pb, pbt = banks[sub]
                                for g in range(g_n):
                                    kc = kg * PG + g
                                    st, sp_ = (kc == 0), (kc == nkc - 1)
                                    T.op("pe", nc.tensor.matmul, KW(pb[:, :], lhsT=act[:, kc, sub * 128:(sub + 1) * 128], rhs=wb[:, g, :], start=st, stop=sp_), r=[acttok(kc), wt], w=[pbt])
                        for sub in range(4):
                            evac(sub, cp, banks[sub][0], banks[sub][1])

                def gemm_feat(name, act, acttok, nkc, cp, evac, ntok=512):
                    nkg = -(-nkc // PG)
                    banks = [PSR.next() for _ in range(4)]
                    for kg in range(nkg):
                        g_n = min(PG, nkc - kg * PG)
                        wb, wt = load_panel(WP, name, cp, kg, g_n)
                        for j in range(4):
                            pb, pbt = banks[j]
                            for g in range(g_n):
                                kc = kg * PG + g
                                st, sp_ = (kc == 0), (kc == nkc - 1)
                                T.op("pe", nc.tensor.matmul, KW(pb[:, 0:ntok], lhsT=wb[:, g, j * 128:(j + 1) * 128], rhs=act[:, kc, 0:ntok], start=st, stop=sp_), r=[acttok(kc), wt], w=[pbt])
                    for j in range(4):
                        evac(j, banks[j][0], banks[j][1])

                def post_norm_residual(grow):
                    gp, gpt = GP.next()
                    T.dma("sp", gp[:], bass.AP(tensor=rows.tensor, offset=grow * D, ap=[[0, 128], [1, D]]), "gp", w=[gpt])
                    for sub in range(4):
                        ss, sst = SS.next()
                        if not OPT_PN:
                            T.op("act", nc.scalar.activation, KW(out=JKo[:], in_=U[:, sub, :], func=AF.Square), r=[("U", sub)], w=["JKo"])
                            T.op("dve", nc.vector.tensor_reduce, KW(out=ss[:, 0:1], in_=JKo[:], axis=AX.X, op=ALU.add), r=["JKo"], w=[sst])
                            T.op("act", nc.scalar.activation, KW(out=ss[:, 1:2], in_=ss[:, 0:1], func=AF.Ln, bias=EPSC, scale=1.0 / D), r=[sst], w=[sst])
                            T.op("act", nc.scalar.activation, KW(out=ss[:, 2:3], in_=ss[:, 1:2], func=AF.Exp, scale=-0.5), r=[sst], w=[sst])
                            T.op("dve", nc.vector.scalar_tensor_tensor, KW(out=JKo[:], in0=U[:, sub, :], scalar=ss[:, 2:3], in1=gp[:], op0=ALU.mult, op1=ALU.mult), r=[("U", sub), sst, gpt], w=["JKo"])
                            T.op("pool", nc.gpsimd.tensor_tensor, KW(out=X[:, sub, :], in0=X[:, sub, :], in1=JKo[:], op=ALU.add), r=["JKo", ("X", sub)], w=[("X", sub)])
                            continue
                        T.op("dve", nc.vector.tensor_reduce, KW(out=ss[:, 0:1], in_=SSP[:, sub * 4:(sub + 1) * 4], axis=AX.X, op=ALU.add), r=[("SSP", sub)], w=[sst])
                        T.op("act", nc.scalar.activation, KW(out=ss[:, 1:2], in_=ss[:, 0:1], func=AF.Ln, bias=EPSC, scale=1.0 / D), r=[sst], w=[sst])
                        T.op("act", nc.scalar.activation, KW(out=ss[:, 2:3], in_=ss[:, 1:2], func=AF.Exp, scale=-0.5), r=[sst], w=[sst])
                        T.op("dve", nc.vector.scalar_tensor_tensor, KW(out=U[:, sub, :], in0=U[:, sub, :], scalar=ss[:, 2:3], in1=gp[:], op0=ALU.mult, op1=ALU.mult), r=[("U", sub), sst, gpt], w=[("U", sub)])
                        T.op("dve", nc.vector.tensor_tensor, KW(out=X[:, sub, :], in0=X[:, sub, :], in1=U[:, sub, :], op=ALU.add), r=[("U", sub), ("X", sub)], w=[("X", sub)])

                def sq_piece(sub, cp, src, srctok):
                    if not OPT_PN:
                        return
                    jq, jqt = JQ.next()
                    if OPT_ACC:
                        T.op("act", nc.scalar.activation, KW(out=jq[:], in_=src, func=AF.Square, accum_out=SSP[:, sub * 4 + cp:sub * 4 + cp + 1]), r=[srctok], w=[jqt, ("SSP", sub)])
                    else:
                        T.op("act", nc.scalar.activation, KW(out=jq[:], in_=src, func=AF.Square), r=[srctok], w=[jqt])
                        T.op("dve", nc.vector.tensor_reduce, KW(out=SSP[:, sub * 4 + cp:sub * 4 + cp + 1], in_=jq[:], axis=AX.X, op=ALU.add), r=[jqt], w=[("SSP", sub)])

                def pre_norm(gcol):
                    for sub in range(4):
                        norm_T((SS, XN), X[:, sub, :], ("X", sub), gcol, HTb, "HT", sub * 128)

                for mb in range(2):
                    xrow = ji * NMEM + mb * 128
                    T.dma("sp", U[:, mb, :], mems[xrow:xrow + 128, :], "memld%d" % mb, w=[("U", mb)])
                    norm_T((SS, XN), U[:, mb, :], ("U", mb), C_GMEM, HTb, "HT", mb * 128)

                def ev_kx(j, pb, pbt):
                    T.op("dve", nc.vector.tensor_copy, KW(out=KXT[:, j, :], in_=pb[:, 0:NMEM]), r=[pbt], w=["KXT"])
                gemm_feat("w_kv_x", HTb, lambda kc: "HT", KC, 0, ev_kx, ntok=NMEM)
                banks = [PSR.next() for _ in range(2)]
                for kg in range(2):
                    wb, wt = load_panel(WP, "w_kv_x", 1, kg)
                    for mb in range(2):
                        pb, pbt = banks[mb]
                        for g in range(PG):
                            kc = kg * PG + g
                            T.op("pe", nc.tensor.matmul, KW(pb[:, :], lhsT=HTb[:, kc, mb * 128:(mb + 1) * 128], rhs=wb[:, g, :], start=(kc == 0), stop=(kc == KC - 1)), r=["HT", wt], w=[pbt])
                for mb in range(2):
                    pb, pbt = banks[mb]
                    T.op("act", nc.scalar.copy, KW(out=VX[:, mb, :], in_=pb[:, :]), r=[pbt], w=["VX"])
                for t in range(L // 512):
                    t0 = t * 512
                    for sub in range(4):
                        T.dma("pool", X[:, sub, :], xs[lo + t0 + sub * 128: lo + t0 + (sub + 1) * 128, :], "xld%d" % sub, w=[("X", sub)])
                    T.dma("sp", SCR[:, 0:8, :], s_ya[:, :, t0:t0 + 512].rearrange("h p t -> p h t"), "scrA", w=[("SCR", i) for i in range(8)])
                    T.dma("sp", SCR[:, 8:16, :], s_yb[:, :, t0:t0 + 512].rearrange("h p t -> p h t"), "scrB", w=[("SCR", i) for i in range(8, 16)])
                    for cp in range(4):
                        resA = {}
                        gabs = {}
                        for j in range(4):
                            gab, gabt = GAB.next()
                            T.dma("pool", gab[:, 0, :], s_gt[cp * 4 + j, :, t0:t0 + 512], "gab%da" % gabt[1], w=[(gabt, 0)])
                            T.dma("pool", gab[:, 1, :], s_gt[16 + cp * 4 + j, :, t0:t0 + 512], "gab%db" % gabt[1], w=[(gabt, 1)])
                            gabs[j] = (gab, gabt)

                        def ev_a(j, pb, pbt, cp=cp, resA=resA, gabs=gabs):
                            fc = cp * 4 + j
                            gab, gabt = gabs[j]
                            mt, mtt = U[:, j, cp * 512:(cp + 1) * 512], ("U", j)
                            T.op("dve", nc.vector.tensor_tensor, KW(out=mt, in0=pb[:, :], in1=gab[:, 0, :], op=ALU.mult), r=[pbt, (gabt, 0)], w=[mtt])
                            resA[j] = (mt, mtt, gab, gabt)

                        def ev_b(j, pb, pbt, cp=cp, resA=resA):
                            fc = cp * 4 + j
                            mt, mtt, gab, gabt = resA[j]
                            mt2, mtt2 = SGt.next()
                            T.op("dve", nc.vector.tensor_tensor, KW(out=mt2[:], in0=pb[:, :], in1=gab[:, 1, :], op=ALU.mult), r=[pbt, (gabt, 1)], w=[mtt2])
                            T.op("dve", nc.vector.tensor_tensor, KW(out=HTb[:, fc, :], in0=mt, in1=mt2[:], op=ALU.add), r=[mtt, mtt2], w=["HT"])
                        gemm_feat("w_ba", SCR, lambda kc: ("SCR", kc), 8, cp, ev_a)
                        gemm_feat("w_bb", SCR[:, 8:16, :], lambda kc: ("SCR", 8 + kc), 8, cp, ev_b)

                    def ev_copy(sub, cp, pb, pbt):
                        T.op("dve", nc.vector.tensor_copy, KW(out=U[:, sub, cp * 512:(cp + 1) * 512], in_=pb[:, :]), r=[pbt], w=[("U", sub)])

                    def ev_u(sub, cp, pb, pbt):
                        ev_copy(sub, cp, pb, pbt)
                        sq_piece(sub, cp, U[:, sub, cp * 512:(cp + 1) * 512], ("U", sub))
                    gemm_tok("w_out", HTb, lambda kc: "HT", KC, 4, ev_u)
                    post_norm_residual(0)
                    pre_norm(C_GPX)

                    def ev_q(j, pb, pbt):
                        T.op("act", nc.scalar.activation, KW(out=QX[:, j, :], in_=pb[:, :], func=AF.Copy, scale=128 ** -0.5), r=[pbt], w=["QX"])
                    gemm_feat("w_q_x", HTb, lambda kc: "HT", KC, 0, ev_q)
                    for hx in range(4):
                        ob, obt = PSR.next()
                        zb_, zbt = PSR.next()
                        for mb in range(2):
                            sbk, sbt_ = PSR.next()
                            pp, ppt = PP.next()
                            T.op("pe", nc.tensor.matmul, KW(sbk[:, :], lhsT=KXT[:, hx, mb * 128:(mb + 1) * 128], rhs=QX[:, hx, :], start=True, stop=True), r=["KXT", "QX"], w=[sbt_])
                            T.op("act", nc.scalar.activation, KW(out=pp[:], in_=sbk[:, :], func=AF.Exp), r=[sbt_], w=[ppt])
                            T.op("pe", nc.tensor.matmul, KW(ob[:, :], lhsT=VX[:, mb, hx * 128:(hx + 1) * 128], rhs=pp[:], start=(mb == 0), stop=(mb == 1)), r=["VX", ppt], w=[obt])
                            T.op("pe", nc.tensor.matmul, KW(zb_[:, :], lhsT=ONES, rhs=pp[:], start=(mb == 0), stop=(mb == 1)), r=[ppt], w=[zbt])
                        T.op("dve", nc.vector.reciprocal, KW(out=RZ[:], in_=zb_[:, :]), r=[zbt], w=["RZ"])
                        T.op("dve", nc.vector.tensor_tensor, KW(out=OX[:, hx, :], in0=ob[:, :], in1=RZ[:], op=ALU.mult), r=[obt, "RZ"], w=["OX"])
                    gemm_tok("w_o_x", OX, lambda kc: "OX", 4, 4, ev_u)
                    post_norm_residual(1)
                    pre_norm(C_GPF)
                    for half, (c_lo, c_hi) in enumerate(((0, 24), (24, 44))):
                        for p in range(c_lo // 4, c_hi // 4):
                            resG = {}

                            def ev_g(j, pb, pbt, resG=resG):
                                sg, sgt = SGt.next()
                                T.op("act", nc.scalar.activation, KW(out=sg[:], in_=pb[:, :], func=AF.Silu), r=[pbt], w=[sgt])
                                resG[j] = (sg, sgt)

                            def ev_up(j, pb, pbt, p=p, resG=resG, c_lo=c_lo):
                                sg, sgt = resG[j]
                                slot = p * 4 + j - c_lo
                                T.op("dve", nc.vector.tensor_tensor, KW(out=SCR[:, slot, :], in0=pb[:, :], in1=sg[:], op=ALU.mult), r=[pbt, sgt], w=[("SCR", slot)])
                            gemm_feat("w_gu", HTb, lambda kc: "HT", KC, p, ev_g)
                            gemm_feat("w_gu", HTb, lambda kc: "HT", KC, 11 + p, ev_up)

                        def ev_d(sub, cp, pb, pbt, half=half):
                            if half == 0:
                                ev_copy(sub, cp, pb, pbt)
                            else:
                                T.op("dve", nc.vector.tensor_tensor, KW(out=U[:, sub, cp * 512:(cp + 1) * 512], in0=pb[:, :], in1=U[:, sub, cp * 512:(cp + 1) * 512], op=ALU.add), r=[pbt, ("U", sub)], w=[("U", sub)])
                                sq_piece(sub, cp, U[:, sub, cp * 512:(cp + 1) * 512], ("U", sub))
                        gemm_tok("w_down", SCR, lambda kc: ("SCR", kc), c_hi - c_lo, 4, ev_d, kc_off=c_lo)
                    post_norm_residual(2)
                    for sub in range(4):
                        T.dma("pool", y[lo + t0 + sub * 128: lo + t0 + (sub + 1) * 128, :], X[:, sub, :], "yst%d" % sub, r=[("X", sub)], w=[("y", t, sub)])
                T.barrier()
            lo += L
            ro += R
        T.emit()
    return nc


def _bucket_np(rel):
    try:
        import jax
        import jax.numpy as jnp
        with jax.default_device(jax.devices("cpu")[0]):
            r = jnp.asarray(rel, dtype=jnp.int32)
            nb, me = 16, 8
            n = jnp.abs(r)
            nf = jnp.maximum(n, 1).astype(jnp.float32)
            large = me + (jnp.log(nf / me) / math.log(128 / me) * (nb - me)).astype(jnp.int32)
            large = jnp.minimum(large, nb - 1)
            return np.asarray(jnp.where(r > 0, nb, 0) + jnp.where(n < me, n, large))
    except Exception:
        rel = np.asarray(rel, np.int64)
        n = np.abs(rel)
        nf = np.maximum(n, 1).astype(np.float32)
        large = 8 + (np.log(nf / np.float32(8)) / np.float32(math.log(16)) * np.float32(8)).astype(np.int32)
        large = np.minimum(large, 15)
        return np.where(rel > 0, 16, 0) + np.where(n < 8, n, large)


def make_consts():
    bk = _bucket_np(640 - np.arange(1280))
    e1h = np.zeros((32, 1280), np.float32)
    e1h[bk, np.arange(1280)] = 1.0
    cmask = np.ones((128, 2048), np.float32)
    cmask[:, ::64] = 0.0
    i = np.arange(128)
    same = (i[:, None] // 64) == (i[None, :] // 64)
    mf = (same & (i[:, None] <= i[None, :])).astype(np.float32)
    mb = (same & (i[:, None] >= i[None, :])).astype(np.float32)
    cb = np.concatenate([np.eye(128, dtype=np.float32), np.ones((128, 128), np.float32), mf, mb], axis=1).astype(ml_dtypes.bfloat16)
    return e1h, cmask, cb


def core_inputs(flip, x_loc_list, x_rem_list, mem_list, P):
    fm = lambda v, n: np.ascontiguousarray(np.asarray(v, np.float32).reshape(n, 128).T)
    w_in = np.asarray(P["w_in"][0], np.float32)
    lbl = np.asarray(P["hgrn_lb_logits"], np.float32)
    rb = np.asarray(P["rel_bias"], np.float32)
    if flip:
        w_in = w_in.copy()
        w_in[:, 4096:5120], w_in[:, 5120:6144] = P["w_in"][0][:, 5120:6144], P["w_in"][0][:, 4096:5120]
        lbl = lbl[::-1]
        rb2 = rb.copy()
        rb2[1:16], rb2[17:32] = rb[17:32], rb[1:16]
        rb = rb2
        x_loc_list = [a[::-1] for a in x_loc_list]
        x_rem_list = [a[::-1] for a in x_rem_list]
    cst = np.zeros((128, C_NCOL), np.float32)
    cst[:, C_GPM:C_GPM + 16] = fm(P["g_pre_mix"][0], 16)
    cst[:, C_GPX:C_GPX + 16] = fm(P["g_pre_x"][0], 16)
    cst[:, C_GPF:C_GPF + 16] = fm(P["g_pre_ffn"][0], 16)
    cst[:, C_GMEM:C_GMEM + 16] = fm(P["g_mem"][0], 16)
    cst[:, C_BM:C_BM + 32] = fm(P["b_merge"][0], 32)
    cst[:, C_LBL:C_LBL + 32] = np.ascontiguousarray(lbl.reshape(2, 2, 8, 128).transpose(3, 0, 1, 2).reshape(128, 32))
    cst[:, C_GSUB] = np.asarray(P["g_subln"][0], np.float32)
    cst[:, C_GHN] = np.asarray(P["g_hgrn_norm"][0], np.float32)
    cst[:, C_EPS] = EPS
    e1h, cmask, cb = make_consts()
    xs = np.ascontiguousarray(np.concatenate([np.asarray(a, np.float32) for a in (list(x_loc_list) + list(x_rem_list)) if a.shape[0] > 0], axis=0))
    m = {
        "xs": xs,
        "mems": np.ascontiguousarray(np.concatenate([np.asarray(a, np.float32) for a in mem_list], axis=0)),
        "w_in": np.ascontiguousarray(w_in),
        "w_ba": np.ascontiguousarray(P["w_branch_a"][0], dtype=np.float32),
        "w_bb": np.ascontiguousarray(P["w_branch_b"][0], dtype=np.float32),
        "w_out": np.ascontiguousarray(P["w_out"][0], dtype=np.float32),
        "w_q_x": np.ascontiguousarray(P["w_q_x"][0], dtype=np.float32),
        "w_kv_x": np.ascontiguousarray(P["w_kv_x"][0], dtype=np.float32),
        "w_o_x": np.ascontiguousarray(P["w_o_x"][0], dtype=np.float32),
        "w_gu": np.ascontiguousarray(P["w_gate_up"][0], dtype=np.float32),
        "w_down": np.ascontiguousarray(P["w_down"][0], dtype=np.float32),
        "cst_f32": cst,
        "rows_f32": np.ascontiguousarray(np.stack([P["g_post_mix"][0], P["g_post_x"][0], P["g_post_ffn"][0]]).astype(np.float32)),
        "lamv": np.ascontiguousarray(np.concatenate([P["lam_q1"][0], P["lam_k1"][0], P["lam_q2"][0], P["lam_k2"][0]]).astype(np.float32).reshape(1, 256)),
        "rel_bias": np.ascontiguousarray(rb),
        "e1h": e1h,
        "cmask": cmask,
        "cst_bf": cb,
    }
    return m


def kernel(**inp):
    P = inp
    xp = np.asarray(inp["x_prompt"], np.float32)
    xsm = np.asarray(inp["x_sample"], np.float32)
    mp = np.asarray(inp["mem_prompt"], np.float32)
    ms = np.asarray(inp["mem_sample"], np.float32)
    NCORE = 8
    SP = xp.shape[1]
    SS_ = xsm.shape[1]
    HS = SS_ // 2
    jobs = [(SP, 0), (SP, 0), (HS, HS)]
    nc = build(jobs)
    in_maps = []
    for c in range(NCORE):
        flip = (c % 2 == 1)
        s = c // 2
        xq = xsm[s]
        if not flip:
            sl, sr = xq[:HS], xq[HS:]
        else:
            sl, sr = xq[HS:], xq[:HS]
        empty = np.zeros((0, D), np.float32)
        in_maps.append(core_inputs(flip, [xp[2 * c], xp[2 * c + 1], sl], [empty, empty, sr], [mp[2 * c], mp[2 * c + 1], ms[s]], P))
    res = run_bass_kernel_spmd(nc, in_maps, core_ids=list(range(NCORE)))
    yp = np.zeros_like(xp)
    ysm = np.zeros_like(xsm)
    for c in range(NCORE):
        yc = np.asarray(res.results[c]["y"], np.float32)
        flip = (c % 2 == 1)
        s = c // 2
        a, b, cc = yc[:SP], yc[SP:2 * SP], yc[2 * SP:]
        if flip:
            a, b, cc = a[::-1], b[::-1], cc[::-1]
            ysm[s, HS:] = cc
        else:
            ysm[s, :HS] = cc
        yp[2 * c] = a
        yp[2 * c + 1] = b
    return (yp, ysm)
```
